# Optimizing a Trainium2 kernel written in Bass

```python
import jax, jax.numpy as jnp
from jax import lax
import numpy as np

D_MODEL = 1024
BATCH = 8
SEQ = 2048
DEPTH = 1

MIX_WIDTH = D_MODEL
HEAD_DIM = 64
ATTN_WIDTH = MIX_WIDTH // 2
ATTN_HEADS = ATTN_WIDTH // HEAD_DIM
SGU_WIDTH = MIX_WIDTH - ATTN_WIDTH
SGU_GROUP_DIM = 64
SGU_GROUPS = SGU_WIDTH // SGU_GROUP_DIM
SGU_CHUNK = 128
DILATED_PATTERNS = ((128, 1), (512, 4), (2048, 16))
BLOCK_Q = 128
ROPE_THETA = 500000.0
ROT_DIM = HEAD_DIM // 4
ROT_HALF = ROT_DIM // 2
D_FF = ((8 * D_MODEL // 3 + 255) // 256) * 256
IN_PROJ_WIDTH = 3 * ATTN_WIDTH + 2 * SGU_WIDTH
RMS_EPS = 1e-6
LN_EPS = 1e-5

kernel_name = "hymba_dilated_attn_gmlp_sandwich_block"


def rms_norm(x, gain):
    xf = x.astype(jnp.float32)
    y = xf * lax.rsqrt(jnp.mean(xf * xf, axis=-1, keepdims=True) + RMS_EPS)
    return (y * gain.astype(jnp.float32)).astype(x.dtype)


def layer_norm(x, gain, bias):
    xf = x.astype(jnp.float32)
    mu = jnp.mean(xf, axis=-1, keepdims=True)
    xc = xf - mu
    y = xc * lax.rsqrt(jnp.mean(xc * xc, axis=-1, keepdims=True) + LN_EPS)
    return (y * gain.astype(jnp.float32) + bias.astype(jnp.float32)).astype(x.dtype)


def partial_rotary(x, positions):
    inv_freq = ROPE_THETA ** (-jnp.arange(0, ROT_DIM, 2, dtype=jnp.float32) / ROT_DIM)
    ang = positions.astype(jnp.float32)[:, :, None, None] * inv_freq
    cos, sin = jnp.cos(ang), jnp.sin(ang)
    xf = x.astype(jnp.float32)
    x1 = xf[..., :ROT_HALF]
    x2 = xf[..., ROT_HALF:ROT_DIM]
    out = jnp.concatenate([x1 * cos - x2 * sin, x2 * cos + x1 * sin, xf[..., ROT_DIM:]], axis=-1)
    return out.astype(x.dtype)


def dilated_window_attention(q, k, v, window, dilation):
    B, H, S, Dh = q.shape
    L = S // dilation
    W = window // dilation
    nblk = -(-L // BLOCK_Q)
    Lp = nblk * BLOCK_Q

    def strided(t):
        return t.reshape(B, H, L, dilation, Dh).transpose(0, 1, 3, 2, 4)

    qs, ks, vs = strided(q), strided(k), strided(v)
    qs = jnp.pad(qs, ((0, 0), (0, 0), (0, 0), (0, Lp - L), (0, 0)))
    kv_pad = ((0, 0), (0, 0), (0, 0), (W, Lp - L), (0, 0))
    kp = jnp.pad(ks, kv_pad)
    vp = jnp.pad(vs, kv_pad)
    blk = jnp.arange(nblk)[:, None]
    col = jnp.arange(BLOCK_Q + W)[None, :]
    idx = blk * BLOCK_Q + col
    kb = kp[:, :, :, idx]
    vb = vp[:, :, :, idx]
    qb = qs.reshape(B, H, dilation, nblk, BLOCK_Q, Dh)

    scale = 1.0 / np.sqrt(HEAD_DIM)
    s = jnp.einsum('bhrnqd,bhrnkd->bhrnqk', qb, kb,
                   preferred_element_type=jnp.float32) * scale
    row = jnp.arange(BLOCK_Q)[:, None]
    dist = row + W - jnp.arange(BLOCK_Q + W)[None, :]
    key_pos = jnp.arange(nblk)[:, None, None] * BLOCK_Q + jnp.arange(BLOCK_Q + W)[None, None, :] - W
    mask = (dist >= 0)[None] & (dist <= W)[None] & (key_pos >= 0)
    s = jnp.where(mask, s, jnp.finfo(jnp.float32).min)
    m = jnp.max(s, axis=-1, keepdims=True)
    p = jnp.exp(s - m)
    denom = jnp.sum(p, axis=-1, keepdims=True)
    o = jnp.einsum('bhrnqk,bhrnkd->bhrnqd', p, vb.astype(jnp.float32)) / denom
    lse = (m + jnp.log(denom))[..., 0]
    o = o.reshape(B, H, dilation, Lp, Dh)[:, :, :, :L]
    lse = lse.reshape(B, H, dilation, Lp)[:, :, :, :L]
    o = o.transpose(0, 1, 3, 2, 4).reshape(B, H, S, Dh)
    lse = lse.transpose(0, 1, 3, 2).reshape(B, H, S)
    return o, lse


def dilated_mixture_attention(q, k, v):
    outs, lses = [], []
    for window, dilation in DILATED_PATTERNS:
        o, lse = dilated_window_attention(q, k, v, window, dilation)
        outs.append(o)
        lses.append(lse)
    wts = jax.nn.softmax(jnp.stack(lses, axis=0), axis=0)
    o = jnp.sum(wts[..., None] * jnp.stack(outs, axis=0), axis=0)
    return o.astype(q.dtype)


def spatial_gating(u, v, ln_gain, ln_bias, w_spatial, b_spatial):
    B, S, _ = u.shape
    u = jax.nn.gelu(u, approximate=False)
    v = layer_norm(jax.nn.gelu(v, approximate=False), ln_gain, ln_bias)
    vc = v.reshape(B, S // SGU_CHUNK, SGU_CHUNK, SGU_GROUPS, SGU_GROUP_DIM)
    causal = jnp.tril(jnp.ones((SGU_CHUNK, SGU_CHUNK), dtype=bool))
    w = jnp.where(causal[None], w_spatial, 0.0).astype(v.dtype)
    mixed = jnp.einsum('gij,bnjgc->bnigc', w, vc) + b_spatial.T[:, :, None].astype(v.dtype)
    return u * mixed.reshape(B, S, SGU_WIDTH)


def setup_inputs(seed: int = 0) -> dict:
    key = jax.random.key(seed)
    ks = jax.random.split(key, 20)
    f32 = jnp.float32

    def nrm(k, shape, scale):
        return jax.random.normal(k, shape, f32) * scale

    def gain(k, width):
        return 1.0 + 0.05 * jax.random.normal(k, (DEPTH, width), f32)

    x = jax.random.normal(ks[0], (BATCH, SEQ, D_MODEL), f32)
    offsets = jax.random.randint(ks[1], (BATCH, 1), 0, 4096, dtype=jnp.int32)
    positions = (jnp.arange(SEQ, dtype=jnp.int32)[None, :] + offsets).astype(jnp.int32)
    return {
        "x": x,
        "positions": positions,
        "pre_mix_norm": gain(ks[2], D_MODEL),
        "w_in": nrm(ks[3], (DEPTH, D_MODEL, IN_PROJ_WIDTH), D_MODEL ** -0.5),
        "sgu_ln_gain": gain(ks[4], SGU_WIDTH),
        "sgu_ln_bias": nrm(ks[5], (DEPTH, SGU_WIDTH), 0.02),
        "sgu_w_spatial": nrm(ks[6], (DEPTH, SGU_GROUPS, SGU_CHUNK, SGU_CHUNK), 0.5 * SGU_CHUNK ** -0.5),
        "sgu_b_spatial": 1.0 + nrm(ks[7], (DEPTH, SGU_GROUPS, SGU_CHUNK), 0.1),
        "attn_out_norm": gain(ks[8], ATTN_WIDTH),
        "sgu_out_norm": gain(ks[9], SGU_WIDTH),
        "w_out": nrm(ks[10], (DEPTH, MIX_WIDTH, D_MODEL), MIX_WIDTH ** -0.5),
        "post_mix_norm": gain(ks[11], D_MODEL),
        "pre_ffn_norm": gain(ks[12], D_MODEL),
        "w_gate": nrm(ks[13], (DEPTH, D_MODEL, D_FF), D_MODEL ** -0.5),
        "w_up": nrm(ks[14], (DEPTH, D_MODEL, D_FF), D_MODEL ** -0.5),
        "w_down": nrm(ks[15], (DEPTH, D_FF, D_MODEL), D_FF ** -0.5),
        "post_ffn_norm": gain(ks[16], D_MODEL),
    }


def reference(x, positions, pre_mix_norm, w_in, sgu_ln_gain, sgu_ln_bias, sgu_w_spatial,
              sgu_b_spatial, attn_out_norm, sgu_out_norm, w_out, post_mix_norm,
              pre_ffn_norm, w_gate, w_up, w_down, post_ffn_norm):
    B, S, _ = x.shape
    for l in range(DEPTH):
        h = rms_norm(x, pre_mix_norm[l])
        proj = h @ w_in[l]
        q, k, v_a, u, v_s = jnp.split(
            proj, [ATTN_WIDTH, 2 * ATTN_WIDTH, 3 * ATTN_WIDTH, 3 * ATTN_WIDTH + SGU_WIDTH], axis=-1)
        q = partial_rotary(q.reshape(B, S, ATTN_HEADS, HEAD_DIM), positions).transpose(0, 2, 1, 3)
        k = partial_rotary(k.reshape(B, S, ATTN_HEADS, HEAD_DIM), positions).transpose(0, 2, 1, 3)
        v_a = v_a.reshape(B, S, ATTN_HEADS, HEAD_DIM).transpose(0, 2, 1, 3)
        attn = dilated_mixture_attention(q, k, v_a)
        attn = attn.transpose(0, 2, 1, 3).reshape(B, S, ATTN_WIDTH)
        sgu = spatial_gating(u, v_s, sgu_ln_gain[l], sgu_ln_bias[l],
                             sgu_w_spatial[l], sgu_b_spatial[l])
        mixed = jnp.concatenate([rms_norm(attn, attn_out_norm[l]),
                                 rms_norm(sgu, sgu_out_norm[l])], axis=-1)
        y = mixed @ w_out[l]
        x = x + rms_norm(y, post_mix_norm[l])
        h = rms_norm(x, pre_ffn_norm[l])
        f = (jax.nn.silu(h @ w_gate[l]) * (h @ w_up[l])) @ w_down[l]
        x = x + rms_norm(f, post_ffn_norm[l])
    return x
```

```python
import numpy as np
from contextlib import ExitStack
import concourse.bass as bass
import concourse.mybir as mybir
from concourse.bass_utils import run_bass_kernel_spmd

F32 = mybir.dt.float32
BF16 = mybir.dt.bfloat16
I32 = mybir.dt.int32
AF = mybir.ActivationFunctionType
ALU = mybir.AluOpType

S = 2048
D = 1024
NT = 16
DFF = 2816
NFF = 22
INW = 2560
RMS_EPS = 1e-6
LN_EPS = 1e-5
TWO_PI = 2.0 * np.pi
C1 = 6.28125
C2 = TWO_PI - 6.28125
PI_LO = 3.1415925

DEBUG = {}
STOP = None
VERBOSE = False
TSTAGE = 0


class Sem:
    def __init__(self, es, nc, name):
        self.sem = es.enter_context(nc.semaphore(name))
        self.cnt = 0
        self.key = name


class Eng:
    def __init__(self, es, nc, eng, name):
        self.e = eng
        self.s = Sem(es, nc, "s_" + name)
        self.seen = {}
        self.name = name

    def wait(self, deps):
        for d in deps:
            if d is None:
                continue
            s, c = d
            if self.seen.get(s.key, 0) >= c:
                continue
            self.e.wait_ge(s.sem, c)
            self.seen[s.key] = c

    def op(self, fn, deps=(), signal=True):
        self.wait(deps)
        ins = fn(self.e)
        if signal:
            self.s.cnt += 1
            ins.then_inc(self.s.sem, 1)
            return (self.s, self.s.cnt)
        return None

    def dma(self, out, in_, dsem, deps=()):
        self.wait(deps)
        ins = self.e.dma_start(out=out, in_=in_)
        dsem.cnt += 16
        ins.then_inc(dsem.sem, 16)
        return (dsem, dsem.cnt)

    def last(self):
        return (self.s, self.s.cnt) if self.s.cnt else None


def build_program():
    nc = bass.Bass("TRN2", target_bir_lowering=False)

    def din(name, shape, dt=F32):
        return nc.dram_tensor(name, list(shape), dt, kind="ExternalInput").ap()

    x_d = din("x", [S, D])
    pos_d = din("pos", [128, NT], I32)
    invf_d = din("invf", [128, 16])
    ident_d = din("ident", [128, 128])
    maskc_d = din("maskc", [128, NT * 128])
    tri_d = din("tri", [128, 128])
    g_premix_d = din("g_premix", [128, D])
    g_postmix_d = din("g_postmix", [128, D])
    g_preffn_d = din("g_preffn", [128, D])
    g_postffn_d = din("g_postffn", [128, D])
    g_mixT_d = din("g_mixT", [128, 8])
    ln_g_d = din("ln_g", [128, 512])
    ln_b_d = din("ln_b", [128, 512])
    w_in_d = din("w_in", [D, INW])
    wsT_d = din("wsT", [128, 8 * 128])
    bsp_d = din("bsp", [128, 8])
    w_out_d = din("w_out", [D, D])
    w_gu_d = din("w_gu", [NFF, 128, 2 * 8 * 128])
    w_down_d = din("w_down", [DFF, D])
    out_d = nc.dram_tensor("out", [S, D], F32, kind="ExternalOutput").ap()
    dbg_d = {}
    for k, (shape, dt) in DEBUG.items():
        dbg_d[k] = nc.dram_tensor("dbg_" + k, list(shape), dt, kind="ExternalOutput").ap()

    w_in_v = w_in_d.rearrange("(kc p) n -> p kc n", p=128)
    w_out_v = w_out_d.rearrange("(kc p) n -> p kc n", p=128)

    with ExitStack() as es0:
        SP = Eng(es0, nc, nc.sync, "sp")
        ACT = Eng(es0, nc, nc.scalar, "act")
        DVE = Eng(es0, nc, nc.vector, "dve")
        PE = Eng(es0, nc, nc.tensor, "pe")
        POOL = Eng(es0, nc, nc.gpsimd, "pool")
        ENGS = [SP, ACT, DVE, PE, POOL]
        dma_sems = []

        def dsem(name):
            s = Sem(es0, nc, "d_" + name)
            dma_sems.append(s)
            return s

        def barrier():
            lasts = [e.last() for e in ENGS] + [(s, s.cnt) for s in dma_sems if s.cnt]
            for e in ENGS:
                e.wait(lasts)

        uid = [0]

        def sb(es, name, shape, dt):
            uid[0] += 1
            return es.enter_context(nc.sbuf_tensor(f"sb_{name}_{uid[0]}", list(shape), dt))

        def ps(es, name, shape, dt=F32):
            uid[0] += 1
            return es.enter_context(nc.psum_tensor(f"pp_{name}_{uid[0]}", list(shape), dt))

        class Carver:
            def __init__(self, flat):
                self.flat = flat
                self.off = 0

            def get(self, shape, dt):
                n = int(np.prod(shape[1:]))
                nf = (n * (4 if dt in (F32, I32) else 2) + 3) // 4
                v = self.flat[:, self.off:self.off + nf]
                self.off += nf
                assert self.off <= self.flat.shape[1]
                if dt != F32:
                    v = v.bitcast(dt)
                if len(shape) == 3:
                    v = v.rearrange("p (a b) -> p a b", b=shape[2])
                elif len(shape) == 4:
                    v = v.rearrange("p (a b c) -> p a b c", b=shape[2], c=shape[3])
                return v

        def dbg_dump(name, src_ap):
            if name in dbg_d:
                barrier()
                s = dsem("dbg_" + name)
                d = SP.dma(dbg_d[name], src_ap, s)
                SP.wait([d])

        bufA = sb(es0, "bufA", [128, NT, 512], F32)
        bufB = sb(es0, "bufB", [128, NT, 512], F32)
        identb = sb(es0, "identb", [128, 128], BF16)
        epsr = sb(es0, "epsr", [128, 1], F32)
        epsl = sb(es0, "epsl", [128, 1], F32)
        halfpi = sb(es0, "halfpi", [128, 1], F32)
        junk = sb(es0, "junk", [128, 1024], BF16)
        stat = sb(es0, "stat", [128, 16 * NT], F32)
        xt = [sb(es0, f"xt{i}", [128, D], F32) for i in range(2)]
        d_xt = [dsem(f"xt{i}") for i in range(2)]
        d_c = dsem("consts")
        d_c2 = dsem("consts2")

        c_deps = []
        c_deps.append(POOL.dma(identb[:], ident_d, d_c))
        i0 = DVE.op(lambda e: e.memset(epsr[:], RMS_EPS))
        i0 = DVE.op(lambda e: e.memset(epsl[:], LN_EPS))
        i0 = DVE.op(lambda e: e.memset(halfpi[:], np.pi / 2))
        setup_dep = i0

        def ST(col, t):
            return stat[:, col * NT + t: col * NT + t + 1]

        h2T = sb(es0, "h2T", [128, 8, 1024], BF16)
        gB3 = sb(es0, "gB3", [128, D], F32)
        d_g3 = dsem("g3")
        g3_dep = SP.dma(gB3[:], g_preffn_d, d_g3)
        d1s = {"h2n_free": [None, None], "ps_free": None, "cnt": 0}

        def d1_front(t, h2n, in_deps):
            b = d1s["cnt"] % 2
            d1s["cnt"] += 1
            s1 = ACT.op(lambda e: e.activation(out=h2n[b][:, 0:512], in_=bufA[:, t, :], func=AF.Square,
                                               accum_out=ST(0, t)), deps=list(in_deps) + [d1s["h2n_free"][b]])
            s2 = ACT.op(lambda e: e.activation(out=h2n[b][:, 512:1024], in_=bufB[:, t, :],
                                               func=AF.Square, accum_out=ST(1, t)))
            ad = DVE.op(lambda e: e.tensor_tensor(out=ST(2, t), in0=ST(0, t), in1=ST(1, t),
                                                  op=ALU.add), deps=[s1, s2])
            q1 = ACT.op(lambda e: e.activation(out=ST(3, t), in_=ST(2, t), func=AF.Sqrt,
                                               scale=1.0 / D, bias=epsr[:]), deps=[ad])
            r1 = DVE.op(lambda e: e.reciprocal(out=ST(4, t), in_=ST(3, t)), deps=[q1])
            m1 = DVE.op(lambda e: e.scalar_tensor_tensor(
                out=h2n[b][:, 0:512], in0=bufA[:, t, :], scalar=ST(4, t), in1=gB3[:, 0:512],
                op0=ALU.mult, op1=ALU.mult), deps=[r1, d1s["h2n_free"][b], g3_dep])
            m2 = DVE.op(lambda e: e.scalar_tensor_tensor(
                out=h2n[b][:, 512:1024], in0=bufB[:, t, :], scalar=ST(4, t), in1=gB3[:, 512:1024],
                op0=ALU.mult, op1=ALU.mult), deps=[r1])
            return (b, [m1, m2])

        def d1_pe(fr, tl, h2n, psv, pe_deps):
            b, mdeps = fr
            PE.wait(mdeps + [d1s["ps_free"]] + list(pe_deps))
            for kc in range(8):
                tr = PE.op(lambda e: e.transpose(out=psv[:, kc, :],
                                                 in_=h2n[b][:, kc * 128:(kc + 1) * 128],
                                                 identity=identb[:]), signal=(kc == 7))
            d1s["h2n_free"][b] = tr
            ev = ACT.op(lambda e: e.activation(out=h2T[:, :, tl * 128:(tl + 1) * 128], in_=psv, func=AF.Copy),
                        deps=[tr])
            d1s["ps_free"] = ev
            return ev

        es1 = es0.enter_context(ExitStack())
        if True:
            vn = sb(es1, "vn", [128, NT, 512], BF16)
            with ExitStack() as es2:
                qkT = sb(es2, "qkT", [128, 8, S], BF16)
                vaug = sb(es2, "vaug", [128, NT, 8, 65], BF16)
                with ExitStack() as es3:
                    w_in = sb(es3, "w_in", [128, 8, INW], BF16)
                    cv = Carver(bufA[:].rearrange("p a b -> p (a b)"))
                    gB1 = cv.get([128, D], F32)
                    lngB = cv.get([128, 512], F32)
                    lnbB = cv.get([128, 512], F32)
                    posi = sb(es3, "posi", [128, NT], I32)
                    posf = sb(es3, "posf", [128, NT], F32)
                    invf = sb(es3, "invf", [128, 16], F32)
                    ang = sb(es3, "ang", [128, NT, 8], F32)
                    kf = sb(es3, "kf", [128, NT, 8], F32)
                    ki = sb(es3, "ki", [128, NT, 8], I32)
                    rr = sb(es3, "rr", [128, NT, 8], F32)
                    ra = sb(es3, "ra", [128, NT, 8], F32)
                    CC = sb(es3, "CC", [128, NT, 2, 8], F32)
                    SS = sb(es3, "SS", [128, NT, 2, 8], F32)
                    xn = [cv.get([128, D], BF16) for i in range(2)]
                    hT = [cv.get([128, 8, 128], BF16) for i in range(2)]
                    qkb = [cv.get([128, 16, 64], BF16) for i in range(2)]
                    T1 = cv.get([128, 16, 16], F32)
                    T2 = cv.get([128, 16, 16], F32)
                    vg = [cv.get([128, 512], F32) for i in range(2)]
                    tmpv = cv.get([128, 512], F32)
                    ps_t = [ps(es3, f"ps_t{i}", [128, 8, 128], BF16) for i in range(1)]
                    ps_tq = [ps(es3, f"ps_tq{i}", [128, 8, 128], BF16) for i in range(1)]
                    ps_qk = ps(es3, "ps_qk", [128, 1024])
                    ps_v = ps(es3, "ps_v", [128, 512])
                    ps_u = ps(es3, "ps_u", [128, 512])
                    ps_vs = ps(es3, "ps_vs", [128, 512])
                    d_w = [dsem(f"win{i}") for i in range(5)]

                    d_cg = dsem("cg1")
                    d_cl = dsem("cln")
                    g1_dep = SP.dma(gB1[:], g_premix_d, d_cg)
                    pos_deps = []
                    ln_deps = []
                    id_dep = c_deps[0]

                    def late_consts():
                        SP.dma(posi[:], pos_d, d_c2)
                        SP.dma(invf[:], invf_d, d_c2)
                        pos_deps.append((d_c2, d_c2.cnt))
                        SP.dma(lngB[:], ln_g_d, d_cl)
                        SP.dma(lnbB[:], ln_b_d, d_cl)
                        ln_deps.append((d_cl, d_cl.cnt))
                    w_deps = []
                    for cg in range(5):
                        w_deps.append(POOL.dma(w_in[:, :, cg * 512:(cg + 1) * 512],
                                               w_in_v[:, :, cg * 512:(cg + 1) * 512], d_w[cg]))
                    ones_dep = DVE.op(lambda e: e.memset(vaug[:, :, :, 64:65], 1.0))

                    tabs = {}

                    def make_tables():
                        a = DVE.op(lambda e: e.tensor_copy(out=posf[:], in_=posi[:]), deps=pos_deps)
                        a = DVE.op(lambda e: e.tensor_tensor(
                            out=ang[:], in0=posf[:].unsqueeze(2).to_broadcast([128, NT, 8]),
                            in1=invf[:, 0:8].unsqueeze(1).to_broadcast([128, NT, 8]), op=ALU.mult), deps=[a])
                        a2 = DVE.op(lambda e: e.tensor_tensor(
                            out=kf[:], in0=posf[:].unsqueeze(2).to_broadcast([128, NT, 8]),
                            in1=invf[:, 8:16].unsqueeze(1).to_broadcast([128, NT, 8]), op=ALU.mult), deps=[a])
                        a = DVE.op(lambda e: e.tensor_tensor(out=ang[:], in0=ang[:], in1=kf[:], op=ALU.add),
                                   deps=[a, a2])
                        a = DVE.op(lambda e: e.tensor_scalar(out=kf[:], in0=ang[:], scalar1=1.0 / TWO_PI,
                                                             scalar2=None, op0=ALU.mult), deps=[a])
                        a = DVE.op(lambda e: e.tensor_copy(out=ki[:], in_=kf[:]), deps=[a])
                        a = DVE.op(lambda e: e.tensor_copy(out=kf[:], in_=ki[:]), deps=[a])
                        a = DVE.op(lambda e: e.scalar_tensor_tensor(out=rr[:], in0=kf[:], scalar=-C1, in1=ang[:],
                                                                    op0=ALU.mult, op1=ALU.add), deps=[a])
                        a = DVE.op(lambda e: e.scalar_tensor_tensor(out=ra[:], in0=kf[:], scalar=-C2, in1=rr[:],
                                                                    op0=ALU.mult, op1=ALU.add), deps=[a])
                        a = DVE.op(lambda e: e.tensor_scalar(out=rr[:], in0=ra[:], scalar1=-PI_LO, scalar2=PI_LO,
                                                             op0=ALU.max, op1=ALU.min), deps=[a])
                        r_dep = a
                        a = DVE.op(lambda e: e.scalar_tensor_tensor(out=ra[:], in0=rr[:], scalar=-1.0, in1=rr[:],
                                                                    op0=ALU.mult, op1=ALU.max), deps=[a])
                        ra_dep = a
                        for j in range(2):
                            tab_dep = ACT.op(lambda e: e.activation(out=SS[:, :, j, :], in_=rr[:], func=AF.Sin),
                                             deps=[r_dep])
                        for j in range(2):
                            tab_dep = ACT.op(lambda e: e.activation(out=CC[:, :, j, :], in_=ra[:], func=AF.Sin,
                                                                    scale=-1.0, bias=halfpi[:]),
                                             deps=[ra_dep, setup_dep])

                        tabs['dep'] = tab_dep

                    if STOP == "S":
                        barrier()
                        return nc
                    if VERBOSE:
                        print('phaseA sbuf remaining', nc.sbuf_bytes_remaining)
                    NTA = NT
                    xt_free = [None, None]
                    xn_free = [None, None]
                    hT_free = [None, None]
                    qkb_free = [None, None]
                    vg_free = [None, None]
                    st = {"pst_free": None, "pstq_free": None, "ln_pending": None, "T_free": None}
                    acc_free = {"qk": [], "v": [], "u": [], "vs": []}
                    ld_dep = {}
                    xn_dep = {}
                    hT_dep = {}

                    def ln_sqrt(tp, bp, sq_dep):
                        return ACT.op(lambda e: e.activation(out=ST(6, tp), in_=ST(5, tp), func=AF.Sqrt,
                                                             scale=1.0 / 512, bias=epsl[:]), deps=[sq_dep, setup_dep])

                    def ln_dve(tp, bp, s1):
                        r1 = DVE.op(lambda e: e.reciprocal(out=ST(7, tp), in_=ST(6, tp)), deps=[s1])
                        t1 = DVE.op(lambda e: e.scalar_tensor_tensor(
                            out=tmpv[:], in0=vg[bp][:], scalar=ST(4, tp), in1=lngB[:],
                            op0=ALU.add, op1=ALU.mult), deps=[r1] + ln_deps)
                        t2 = DVE.op(lambda e: e.scalar_tensor_tensor(
                            out=vn[:, tp, :], in0=tmpv[:], scalar=ST(7, tp), in1=lnbB[:],
                            op0=ALU.mult, op1=ALU.add), deps=[t1])
                        return t2

                    def ln_finish(tp, bp, sq_dep):
                        return ln_dve(tp, bp, ln_sqrt(tp, bp, sq_dep))

                    sqj = sb(es3, "sqj", [128, D], BF16)
                    xt3 = [xt[0], xt[1], cv.get([128, D], F32)]
                    d_xt3 = d_xt + [dsem("xt2")]
                    xt_free = [None, None, None]

                    def load_x(t):
                        b3 = t % 3
                        ld_dep[t] = SP.dma(xt3[b3][:], x_d[t * 128:(t + 1) * 128, :], d_xt3[b3],
                                           deps=[xt_free[b3], xn_dep.get(t - 3)])

                    rs_dep = {}

                    def front_stats(t):
                        b = t % 2
                        b3 = t % 3
                        ld = ld_dep[t]
                        xnd = DVE.op(lambda e: e.tensor_tensor(out=xn[b][:], in0=xt3[b3][:], in1=gB1[:],
                                                               op=ALU.mult), deps=[ld, xn_free[b], g1_dep])
                        xn_dep[t] = xnd
                        sq = ACT.op(lambda e: e.activation(out=sqj[:], in_=xt3[b3][:], func=AF.Square,
                                                           accum_out=ST(0, t)), deps=[ld])
                        xt_free[b3] = sq
                        sd = ACT.op(lambda e: e.activation(out=ST(1, t), in_=ST(0, t), func=AF.Sqrt,
                                                           scale=1.0 / D, bias=epsr[:]), deps=[sq, setup_dep])
                        lp = st["ln_pending"]
                        s1 = None
                        if lp is not None:
                            s1 = ln_sqrt(*lp)
                        rs_dep[t] = DVE.op(lambda e: e.reciprocal(out=ST(2, t), in_=ST(1, t)), deps=[sd])
                        if lp is not None:
                            vg_free[lp[1]] = ln_dve(lp[0], lp[1], s1)
                            st["ln_pending"] = None

                    def front_pe(t):
                        b = t % 2
                        PE.wait([xn_dep[t], st["pst_free"], id_dep])
                        for kc in range(8):
                            tr = PE.op(lambda e: e.transpose(out=ps_t[0][:, kc, :],
                                                             in_=xn[b][:, kc * 128:(kc + 1) * 128],
                                                             identity=identb[:]), signal=(kc == 7))
                        xn_free[b] = tr
                        ev = ACT.op(lambda e: e.activation(out=hT[b][:], in_=ps_t[0][:], func=AF.Copy),
                                    deps=[tr, hT_free[b]])
                        st["pst_free"] = ev
                        hT_dep[t] = ev

                    banks = [(ps_qk[:, 0:512], "qk"), (ps_qk[:, 512:1024], "qk"), (ps_v[:], "v"),
                             (ps_u[:], "u"), (ps_vs[:], "vs")]
                    mm_deps = {}

                    def mm(t, cgs):
                        b = t % 2
                        for cg in cgs:
                            ap_out, key = banks[cg]
                            PE.wait([hT_dep[t], w_deps[cg]] + acc_free[key])
                            for kc in range(8):
                                m = PE.op(lambda e: e.matmul(out=ap_out, lhsT=hT[b][:, kc, :],
                                                             rhs=w_in[:, kc, cg * 512:(cg + 1) * 512],
                                                             start=(kc == 0), stop=(kc == 7)),
                                          signal=(kc == 7))
                            mm_deps[(t, cg)] = m
                        if 4 in cgs:
                            hT_free[b] = mm_deps[(t, 4)]

                    qk_ready = {}
                    junk_free = [None, None]

                    def back_qkv(t):
                        b = t % 2
                        qkv = ps_qk[:].rearrange("p (h d) -> p h d", d=64)
                        md = mm_deps[(t, 1)]
                        t1 = DVE.op(lambda e: e.scalar_tensor_tensor(
                            out=T1[:], in0=qkv[:, :, 0:16], scalar=ST(2, t),
                            in1=CC[:, t, :, :].rearrange("p a b -> p (a b)").unsqueeze(1).to_broadcast([128, 16, 16]),
                            op0=ALU.mult, op1=ALU.mult), deps=[md, tabs['dep'], st["T_free"], rs_dep[t]])
                        t2 = DVE.op(lambda e: e.scalar_tensor_tensor(
                            out=T2[:], in0=qkv[:, :, 0:16], scalar=ST(2, t),
                            in1=SS[:, t, :, :].rearrange("p a b -> p (a b)").unsqueeze(1).to_broadcast([128, 16, 16]),
                            op0=ALU.mult, op1=ALU.mult), deps=[md, tabs['dep']])
                        cp = DVE.op(lambda e: e.tensor_scalar(out=qkb[b][:, :, 16:64], in0=qkv[:, :, 16:64],
                                                              scalar1=ST(2, t), scalar2=None, op0=ALU.mult),
                                    deps=[md, qkb_free[b], t1, t2, rs_dep[t]])
                        o1 = DVE.op(lambda e: e.tensor_tensor(out=qkb[b][:, :, 0:8], in0=T1[:, :, 0:8],
                                                              in1=T2[:, :, 8:16], op=ALU.subtract),
                                    deps=[t1, t2, qkb_free[b]])
                        o2 = DVE.op(lambda e: e.tensor_tensor(out=qkb[b][:, :, 8:16], in0=T1[:, :, 8:16],
                                                              in1=T2[:, :, 0:8], op=ALU.add),
                                    deps=[t1, t2])
                        st["T_free"] = o2
                        acc_free["qk"] = [cp, t1, t2]
                        vcp = DVE.op(lambda e: e.tensor_scalar(
                            out=vaug[:, t, :, 0:64], in0=ps_v[:].rearrange("p (h d) -> p h d", d=64),
                            scalar1=ST(2, t), scalar2=None, op0=ALU.mult),
                            deps=[mm_deps[(t, 2)], ones_dep, rs_dep[t]])
                        acc_free["v"] = [vcp]
                        qk_ready[t] = [cp, o1, o2]

                    def back_rest(t):
                        b = t % 2
                        gu_ = ACT.op(lambda e: e.activation(out=bufB[:, t, :], in_=ps_u[:], func=AF.Gelu,
                                                            scale=ST(2, t)), deps=[mm_deps[(t, 3)], rs_dep[t]])
                        acc_free["u"] = [gu_]
                        gv = ACT.op(lambda e: e.activation(out=vg[b][:], in_=ps_vs[:], func=AF.Gelu, scale=ST(2, t),
                                                           accum_out=ST(3, t)), deps=[mm_deps[(t, 4)], vg_free[b]])
                        acc_free["vs"] = [gv]
                        nm = DVE.op(lambda e: e.tensor_scalar(out=ST(4, t), in0=ST(3, t), scalar1=-1.0 / 512,
                                                              scalar2=None, op0=ALU.mult), deps=[gv])
                        sq2 = ACT.op(lambda e: e.activation(out=junk[:, b * 512:(b + 1) * 512], in_=vg[b][:],
                                                            func=AF.Square, bias=ST(4, t), accum_out=ST(5, t)),
                                     deps=[nm, junk_free[b]])
                        junk_free[b] = sq2
                        st["ln_pending"] = (t, b, sq2)
                        PE.wait(qk_ready[t] + [st["pstq_free"]])
                        for c in range(8):
                            trq = PE.op(lambda e: e.transpose(
                                out=ps_tq[0][:, c, :],
                                in_=qkb[b][:, 2 * c:2 * c + 2, :].rearrange("p a b -> p (a b)"),
                                identity=identb[:]), signal=(c == 7))
                        qkb_free[b] = trq
                        evq = DVE.op(lambda e: e.tensor_copy(out=qkT[:, :, t * 128:(t + 1) * 128],
                                                             in_=ps_tq[0][:]), deps=[trq])
                        st["pstq_free"] = evq

                    load_x(0)
                    load_x(1)
                    load_x(2)
                    late_consts()
                    front_stats(0)
                    front_pe(0)
                    front_stats(1)
                    make_tables()
                    for t in range(NTA):
                        mm(t, [0, 1, 2])
                        if t + 2 < NTA:
                            front_stats(t + 2)
                        elif st["ln_pending"] is not None:
                            lp = st["ln_pending"]
                            vg_free[lp[1]] = ln_finish(*lp)
                            st["ln_pending"] = None
                        back_qkv(t)
                        if t + 1 < NTA:
                            front_pe(t + 1)
                        if t + 3 < NTA:
                            load_x(t + 3)
                        mm(t, [3, 4])
                        back_rest(t)
                    maskb = xt[1][:].bitcast(BF16).rearrange("p (a b) -> p a b", b=128)
                    d_m = dsem("mask")
                    mdep = POOL.dma(xt[1][:].bitcast(BF16), maskc_d, d_m,
                                    deps=[xt_free[1], xn_dep[13], xt_free[0], xt_free[2]])
                    ln_pending = st["ln_pending"]
                    barrier()
                ln_last = ln_finish(*ln_pending)
                if STOP == "A0":
                    return nc
                dbg_dump("qkT", qkT[:].rearrange("p a b -> p (a b)"))
                dbg_dump("vaug", vaug[:].rearrange("p a b c -> p (a b c)"))
                dbg_dump("vn", vn[:].rearrange("p a b -> p (a b)"))
                dbg_dump("gu", bufB[:].rearrange("p a b -> p (a b)"))

                if STOP == "A":
                    return nc
                with ExitStack() as es3:
                    pT = [sb(es3, f"pT{i}", [128, 2, 512], BF16) for i in range(5)]
                    rec = sb(es3, "rec", [128, 2, 4], F32)
                    ps_s = [ps(es3, f"ps_s{i}", [128, 2, 512]) for i in range(3)]
                    acc = ps(es3, "acc", [128, 2, 512])
                    if VERBOSE:
                        print('phaseB sbuf remaining', nc.sbuf_bytes_remaining)
                    wsT_f = xt[0][:].rearrange("p (a b) -> p a b", b=128)
                    junkf = junk[:].bitcast(F32)
                    trib_f = junkf[:, 0:128]
                    bsp = junkf[:, 128:136]
                    gmT = junkf[:, 136:144]
                    d_c3 = dsem("c3")
                    SP.dma(xt[0][:], wsT_d, d_c3)
                    SP.dma(trib_f, tri_d, d_c3)
                    SP.dma(bsp, bsp_d, d_c3)
                    SP.dma(gmT, g_mixT_d, d_c3)
                    cw = [(d_c3, d_c3.cnt)]
                    stb = {"acc_ready": None}
                    pss_free = [None, None, None]
                    pT_free = [None, None, None, None, None]
                    steps = []
                    last_of_group = {}
                    for hp in range(4):
                        for G in range(4):
                            big = list(range(0, 4 * G))
                            small = list(range(4 * G, 4 * G + 4))
                            order = []
                            if big:
                                order.append(big.pop(0))
                            else:
                                order.append(small.pop(0))
                            while big or small:
                                if small:
                                    order.append(small.pop())
                                if big:
                                    order.append(big.pop(0))
                                if big:
                                    order.append(big.pop(0))
                            for m in order:
                                steps.append((hp, G, m))
                            last_of_group[(hp, G)] = order[-1]
                    info = {}

                    def emit_pv(si):
                        hp, G, m = steps[si]
                        slot, lst, mk_dep = info[si]
                        PE.wait([mk_dep, stb["acc_ready"]] if m == 0 else [mk_dep])
                        last = None
                        for idx, (hh, j, jq, m_) in enumerate(lst):
                            first = (m_ == 0 and j == 0)
                            last = PE.op(lambda e: e.matmul(
                                out=acc[:, hh % 2, jq * 65:(jq + 1) * 65],
                                lhsT=pT[slot][:, hh % 2, j * 128:(j + 1) * 128],
                                rhs=vaug[:, m_, hh, :], start=first, stop=False, skip_group_check=True),
                                signal=(idx == len(lst) - 1))
                        pT_free[slot] = last
                        if m == last_of_group[(hp, G)]:
                            accv = acc[:, :, 0:260].rearrange("p h (q c) -> p h q c", c=65)
                            rc = DVE.op(lambda e: e.reciprocal(out=rec[:], in_=accv[:, :, :, 64]),
                                        deps=[last, ln_last])
                            fo = DVE.op(lambda e: e.tensor_tensor(
                                out=bufA[:, 4 * G:4 * G + 4, hp * 128:(hp + 1) * 128]
                                .rearrange("p q (h c) -> p q h c", c=64),
                                in0=accv.rearrange("p h q c -> p q h c")[:, :, :, 0:64],
                                in1=rec[:].rearrange("p h q -> p q h").unsqueeze(3).to_broadcast([128, 4, 2, 64]),
                                op=ALU.mult), deps=[rc])
                            stb["acc_ready"] = fo

                    LAG = 3
                    for si, (hp, G, m) in enumerate(steps):
                        n0 = max(m, 4 * G)
                        cnt = 4 * G + 4 - n0
                        N = 128 * cnt
                        d0 = n0 - m
                        sl = si % 3
                        pl = si % 5
                        PE.wait([pss_free[sl]])
                        for hh in range(2):
                            hb = 64 * hh
                            qk = PE.op(lambda e: e.matmul(
                                out=ps_s[sl][:, hh, 0:N],
                                lhsT=qkT[hb:hb + 64, 4 + hp, m * 128:(m + 1) * 128],
                                rhs=qkT[hb:hb + 64, hp, n0 * 128:(4 * G + 4) * 128],
                                start=True, stop=True), signal=(hh == 1))
                        if si >= LAG:
                            emit_pv(si - LAG)
                        ex = ACT.op(lambda e: e.activation(out=pT[pl][:, :, 0:N], in_=ps_s[sl][:, :, 0:N],
                                                           func=AF.Exp, scale=0.125),
                                    deps=[qk, pT_free[pl]])
                        pss_free[sl] = ex
                        mk = DVE.op(lambda e: e.tensor_tensor(
                            out=pT[pl][:, :, 0:N], in0=pT[pl][:, :, 0:N],
                            in1=maskb[:, d0:d0 + cnt, :].rearrange("p a b -> p (a b)").unsqueeze(1)
                            .to_broadcast([128, 2, N]), op=ALU.mult), deps=[ex, mdep])
                        lst = []
                        for hh in range(2):
                            for j in range(cnt):
                                lst.append((2 * hp + hh, j, n0 + j - 4 * G, m))
                        info[si] = (pl, lst, mk)
                    for si in range(len(steps) - LAG, len(steps)):
                        emit_pv(si)
                    barrier()
                dbg_dump("attn", bufA[:].rearrange("p a b -> p (a b)"))
                if STOP == "B":
                    return nc
            with ExitStack() as es2:
                w_out = sb(es2, "w_out", [128, 8, D], BF16)
                wsT = sb(es2, "wsT", [128, 8, 128], BF16)
                trib = sb(es2, "trib", [128, 128], BF16)
                gB2 = sb(es2, "gB2", [128, D], F32)
                mixb = [sb(es2, f"mixb{i}", [128, D], BF16) for i in range(2)]
                mT = [sb(es2, f"mT{i}", [128, 8, 128], BF16) for i in range(2)]
                tmpy = [sb(es2, f"tmpy{i}", [128, D], F32) for i in range(2)]
                h2n_c = [sb(es2, f"h2nc{i}", [128, D], BF16) for i in range(2)]
                ps_sp = [ps(es2, f"ps_sp{i}", [128, 8, 64]) for i in range(2)]
                ps_tm = [ps(es2, f"ps_tm{i}", [128, 8, 128], BF16) for i in range(2)]
                ps_y = [ps(es2, f"ps_y{i}", [128, D]) for i in range(2)]
                if VERBOSE:
                    print('phaseC sbuf remaining', nc.sbuf_bytes_remaining)
                d_c4 = dsem("c4")
                d_c3b = dsem("c3b")
                bsp_dep = cw
                SP.dma(gB2[:], g_postmix_d, d_c4)
                cs = [(d_c4, d_c4.cnt)] + cw
                cwo = [POOL.dma(w_out[:], w_out_v, d_c3b)]
                wm = DVE.op(lambda e: e.tensor_tensor(
                    out=wsT[:], in0=wsT_f, in1=trib_f.unsqueeze(1).to_broadcast([128, 8, 128]),
                    op=ALU.mult), deps=cw)
                def stat_cols(col, t0, n):
                    return stat[:, col * NT + t0: col * NT + t0 + n]

                scr = [None, None]
                scr2 = [None, None]
                for t in range(NT):
                    sa = ACT.op(lambda e: e.activation(out=h2n_c[t % 2][:, 0:512], in_=bufA[:, t, :],
                                                       func=AF.Square, accum_out=ST(8, t)), deps=[scr[t % 2]])
                    scr[t % 2] = sa
                qa = ACT.op(lambda e: e.activation(out=stat_cols(10, 0, NT), in_=stat_cols(8, 0, NT), func=AF.Sqrt,
                                                   scale=1.0 / 512, bias=epsr[:]), deps=[sa])
                sp_free = [None, None]
                spd = None
                sp_half = [None, None]
                rs_half = [None, None]
                qs_half = [None, None]
                k = 0
                mixb_free = [None, None]
                mixd = {}

                def frontc_stats(t):
                    b = t % 2
                    m1 = ACT.op(lambda e: e.activation(out=mixb[b][:, 0:512], in_=bufA[:, t, :], func=AF.Copy,
                                                       scale=ST(12, t)), deps=[ra_dep, mixb_free[b]])
                    m2 = DVE.op(lambda e: e.tensor_scalar(out=mixb[b][:, 512:1024], in0=bufB[:, t, :],
                                                          scalar1=ST(13, t), scalar2=None, op0=ALU.mult),
                                deps=[rs_half[t // 8], mixb_free[b]])
                    mixd[t] = [m1, m2]

                mT_free = [None, None]
                pstm_free = [None, None]
                mTd = {}

                def frontc_pe(t):
                    b = t % 2
                    PE.wait(mixd[t] + [pstm_free[b]])
                    for kc in range(8):
                        tr = PE.op(lambda e: e.transpose(out=ps_tm[b][:, kc, :],
                                                         in_=mixb[b][:, kc * 128:(kc + 1) * 128],
                                                         identity=identb[:]), signal=(kc == 7))
                    mixb_free[b] = tr
                    ev = ACT.op(lambda e: e.activation(out=mT[b][:], in_=ps_tm[b][:], func=AF.Copy),
                                deps=[tr, mT_free[b]])
                    pstm_free[b] = ev
                    mTd[t] = ev

                for hf in range(2):
                    for g in range(8):
                        sl = k % 2
                        mm = PE.op(lambda e: e.matmul(out=ps_sp[sl][:], lhsT=wsT[:, g, :],
                                                      rhs=vn[:, hf * 8:(hf + 1) * 8, g * 64:(g + 1) * 64],
                                                      start=True, stop=True), deps=[wm, sp_free[sl]])
                        spd = DVE.op(lambda e: e.scalar_tensor_tensor(
                            out=bufB[:, hf * 8:(hf + 1) * 8, g * 64:(g + 1) * 64], in0=ps_sp[sl][:],
                            scalar=bsp[:, g:g + 1], in1=bufB[:, hf * 8:(hf + 1) * 8, g * 64:(g + 1) * 64],
                            op0=ALU.add, op1=ALU.mult), deps=[mm] + bsp_dep)
                        sp_free[sl] = spd
                        k += 1
                    sp_half[hf] = spd
                    if hf == 1:
                        ra_dep = DVE.op(lambda e: e.reciprocal(out=stat_cols(12, 0, NT), in_=stat_cols(10, 0, NT)),
                                        deps=[qa])
                        rs_half[0] = DVE.op(lambda e: e.reciprocal(out=stat_cols(13, 0, 8),
                                                                   in_=stat_cols(11, 0, 8)), deps=[qs_half[0]])
                        for t in range(2):
                            m1 = ACT.op(lambda e: e.activation(
                                out=mixb[t % 2][:, 0:512], in_=bufA[:, t, :], func=AF.Copy, scale=ST(12, t)),
                                deps=[ra_dep])
                            m2 = DVE.op(lambda e: e.tensor_scalar(out=mixb[t % 2][:, 512:1024], in0=bufB[:, t, :],
                                                                  scalar1=ST(13, t), scalar2=None, op0=ALU.mult),
                                        deps=[rs_half[0]])
                            mixd[t] = [m1, m2]
                        frontc_pe(0)
                    for t in range(hf * 8, hf * 8 + 8):
                        ss_ = ACT.op(lambda e: e.activation(out=h2n_c[t % 2][:, 512:1024], in_=bufB[:, t, :],
                                                            func=AF.Square, accum_out=ST(9, t)),
                                     deps=[spd, scr2[t % 2]])
                        scr2[t % 2] = ss_
                    qs_half[hf] = ACT.op(lambda e: e.activation(
                        out=stat_cols(11, hf * 8, 8), in_=stat_cols(9, hf * 8, 8),
                        func=AF.Sqrt, scale=1.0 / 512, bias=epsr[:]), deps=[ss_])
                sp_last_mm = mm
                for kc in range(8):
                    wsc = DVE.op(lambda e: e.tensor_scalar(out=w_out[:, kc, :], in0=w_out[:, kc, :],
                                                           scalar1=gmT[:, kc:kc + 1], scalar2=None, op0=ALU.mult),
                                 deps=cwo + cs)
                cwo = [wsc]
                rs_half[1] = DVE.op(lambda e: e.reciprocal(out=stat_cols(13, 8, 8), in_=stat_cols(11, 8, 8)),
                                    deps=[qs_half[1]])
                cvD = Carver(vn[:].rearrange("p a b -> p (a b)").bitcast(F32))
                gu = [cvD.get([128, 2, 8, 128], BF16) for i in range(2)]
                gB4 = cvD.get([128, D], F32)
                sg = [cvD.get([128, 512], F32) for i in range(2)]
                d_gu = [dsem(f"gu{i}") for i in range(2)]
                d_c5 = dsem("c5")
                ffn = {"gu_idx": 0, "gu_free": [sp_last_mm, sp_last_mm], "pending_w": []}

                def issue_gu(f):
                    sl = ffn["gu_idx"] % 2
                    dep = POOL.dma(gu[sl][:].rearrange("p a b c -> p (a b c)"), w_gu_d[f], d_gu[sl],
                                   deps=[ffn["gu_free"][sl]])
                    ffn["gu_idx"] += 1
                    return (sl, dep)

                ffn["pending_w"].append(issue_gu(0))
                csD = [SP.dma(gB4[:], g_postffn_d, d_c5, deps=[sp_last_mm])]
                dbg_dump("sgu", bufB[:].rearrange("p a b -> p (a b)"))
                psy_free = [None, None]
                xt_free = [None, None]
                stc = {"tmpy_free": None}
                ldc = {}
                mmd = {}

                def loadc(t):
                    b = t % 2
                    ldc[t] = SP.dma(xt[b][:], x_d[t * 128:(t + 1) * 128, :], d_xt[b], deps=[xt_free[b], wm])

                def mmc(t, cg):
                    b = t % 2
                    PE.wait([mTd[t], psy_free[b]] + cwo)
                    for kc in range(8):
                        mm_ = PE.op(lambda e: e.matmul(out=ps_y[b][:, cg * 512:(cg + 1) * 512],
                                                       lhsT=mT[b][:, kc, :],
                                                       rhs=w_out[:, kc, cg * 512:(cg + 1) * 512],
                                                       start=(kc == 0), stop=(kc == 7)),
                                    signal=(kc == 7))
                    mmd[(t, cg)] = mm_
                    if cg == 1:
                        mT_free[b] = mm_

                def backc(t):
                    b = t % 2
                    mm_ = mmd[(t, 1)]
                    sy = ACT.op(lambda e: e.activation(out=tmpy[b][:], in_=ps_y[b][:], func=AF.Square,
                                                       accum_out=ST(14, t)), deps=[mm_, tmpy_free[b]])
                    qy = ACT.op(lambda e: e.activation(out=ST(15, t), in_=ST(14, t), func=AF.Sqrt,
                                                       scale=1.0 / D, bias=epsr[:]), deps=[sy])
                    ry = DVE.op(lambda e: e.reciprocal(out=ST(14, t), in_=ST(15, t)), deps=[qy])
                    ty = DVE.op(lambda e: e.scalar_tensor_tensor(
                        out=tmpy[b][:], in0=ps_y[b][:], scalar=ST(14, t), in1=gB2[:],
                        op0=ALU.mult, op1=ALU.mult), deps=[ry, tmpy_free[b]])
                    psy_free[b] = ty
                    xa = DVE.op(lambda e: e.tensor_tensor(out=bufA[:, t, :], in0=tmpy[b][:, 0:512],
                                                          in1=xt[b][:, 0:512], op=ALU.add),
                                deps=[ty, ldc[t], mixb_free[b]])
                    xb = DVE.op(lambda e: e.tensor_tensor(out=bufB[:, t, :], in0=tmpy[b][:, 512:1024],
                                                          in1=xt[b][:, 512:1024], op=ALU.add), deps=[ty])
                    tmpy_free[b] = xb
                    xt_free[b] = xb
                    xdone[t] = xb

                tmpy_free = [None, None]
                xdone = {}
                psv_c = ps_sp[0][:].rearrange("p a b -> p (a b)").bitcast(BF16).rearrange("p (a b) -> p a b", b=128)
                loadc(0)
                loadc(1)
                frontc_stats(2)
                fr_pend = None
                for t in range(NT):
                    if t + 1 < NT:
                        frontc_pe(t + 1)
                    mmc(t, 0)
                    mmc(t, 1)
                    if t + 3 < NT:
                        frontc_stats(t + 3)
                    if fr_pend is not None:
                        d1_pe(fr_pend[0], fr_pend[1], h2n_c, psv_c, [spd])
                        fr_pend = None
                    jd1 = {1: 0, 3: 1, 5: 2, 7: 3, 9: 4, 11: 5, 12: 6, 13: 7}.get(t)
                    if jd1 is not None:
                        fr_pend = (d1_front(jd1, h2n_c, [xdone[jd1]]), jd1)
                    backc(t)
                    if t + 2 < NT:
                        loadc(t + 2)
                if fr_pend is not None:
                    d1_pe(fr_pend[0], fr_pend[1], h2n_c, psv_c, [spd])
                barrier()
        dbg_dump("x1a", bufA[:].rearrange("p a b -> p (a b)"))
        dbg_dump("x1b", bufB[:].rearrange("p a b -> p (a b)"))
        if STOP == "C":
            return nc
        with ExitStack() as es1:
            wd = sb(es1, "wd", [128, NFF, D], BF16)
            aT = sb(es1, "aT", [128, NFF, 1024], BF16)
            h2n = [sb(es1, f"h2n{i}", [128, D], BF16) for i in range(2)]
            ot = xt
            if VERBOSE:
                print('phaseD sbuf remaining', nc.sbuf_bytes_remaining)
            d_wd = dsem("wd")
            d_o = [dsem(f"o{i}") for i in range(2)]
            cs = csD
            gu_free = ffn["gu_free"]
            o_free = [None, None]
            out_deps = []
            ps_g = [ps(es1, f"ps_g{i}", [128, 512]) for i in range(2)]
            ps_u2 = [ps(es1, f"ps_u2{i}", [128, 512]) for i in range(2)]
            ps_f = [ps(es1, f"ps_f{i}", [128, D]) for i in range(2)]
            psg_free = [None, None]
            psu_free = [None, None]
            sg_free = [None, None]
            psf_free = [None, None]
            pending_w = ffn["pending_w"]
            k = 0
            a_dep = None
            h2T_dep = None
            wd_deps = []
            for hf in range(2):
                T0 = hf * 8
                for f in range(NFF):
                    if f + 1 < NFF:
                        pending_w.append(issue_gu(f + 1))
                    elif hf == 0:
                        pending_w.append(issue_gu(0))
                    if hf == 0:
                        wd_deps.append(POOL.dma(wd[:, f, :], w_down_d[f * 128:(f + 1) * 128, :], d_wd))
                    sl, wdep = pending_w.pop(0)
                    for tg in range(2):
                        kk = k % 2
                        PE.wait([wdep, h2T_dep, psg_free[kk]])
                        for kc in range(8):
                            mg = PE.op(lambda e: e.matmul(out=ps_g[kk][:], lhsT=gu[sl][:, 0, kc, :],
                                                          rhs=h2T[:, kc, tg * 512:(tg + 1) * 512],
                                                          start=(kc == 0), stop=(kc == 7)), signal=(kc == 7))
                        PE.wait([psu_free[kk]])
                        for kc in range(8):
                            mu = PE.op(lambda e: e.matmul(out=ps_u2[kk][:], lhsT=gu[sl][:, 1, kc, :],
                                                          rhs=h2T[:, kc, tg * 512:(tg + 1) * 512],
                                                          start=(kc == 0), stop=(kc == 7)), signal=(kc == 7))
                        si = ACT.op(lambda e: e.activation(out=sg[kk][:], in_=ps_g[kk][:], func=AF.Silu),
                                    deps=[mg, sg_free[kk]])
                        psg_free[kk] = si
                        a_dep = DVE.op(lambda e: e.tensor_tensor(
                            out=aT[:, f, tg * 512:(tg + 1) * 512], in0=sg[kk][:], in1=ps_u2[kk][:],
                            op=ALU.mult), deps=[si, mu])
                        psu_free[kk] = a_dep
                        sg_free[kk] = a_dep
                        k += 1
                    gu_free[sl] = mu
                for tl in range(8):
                    t = T0 + tl
                    b = tl % 2
                    fr = None
                    if hf == 0:
                        fr = d1_front(8 + tl, h2n, [])
                    PE.wait([a_dep, psf_free[b], (d_wd, d_wd.cnt)])
                    for cg in range(2):
                        for f in range(NFF):
                            mm = PE.op(lambda e: e.matmul(out=ps_f[b][:, cg * 512:(cg + 1) * 512],
                                                          lhsT=aT[:, f, tl * 128:(tl + 1) * 128],
                                                          rhs=wd[:, f, cg * 512:(cg + 1) * 512],
                                                          start=(f == 0), stop=(f == NFF - 1)),
                                       signal=(cg == 1 and f == NFF - 1))
                    if fr is not None:
                        psv_d = ps_g[0][:].bitcast(BF16).rearrange("p (a b) -> p a b", b=128)
                        h2T_dep = d1_pe(fr, tl, h2n, psv_d, [psg_free[0], psg_free[1], mu])
                        psg_free[0] = h2T_dep
                    sy = ACT.op(lambda e: e.activation(out=ot[b][:], in_=ps_f[b][:], func=AF.Square,
                                                       accum_out=ST(5, t)), deps=[mm, o_free[b]])
                    qy = ACT.op(lambda e: e.activation(out=ST(6, t), in_=ST(5, t), func=AF.Sqrt,
                                                       scale=1.0 / D, bias=epsr[:]), deps=[sy])
                    ry = DVE.op(lambda e: e.reciprocal(out=ST(7, t), in_=ST(6, t)), deps=[qy])
                    ty = DVE.op(lambda e: e.scalar_tensor_tensor(
                        out=ot[b][:], in0=ps_f[b][:], scalar=ST(7, t), in1=gB4[:],
                        op0=ALU.mult, op1=ALU.mult), deps=[ry, o_free[b]] + cs)
                    psf_free[b] = ty
                    xa = DVE.op(lambda e: e.tensor_tensor(out=ot[b][:, 0:512], in0=ot[b][:, 0:512],
                                                          in1=bufA[:, t, :], op=ALU.add), deps=[ty])
                    xb = DVE.op(lambda e: e.tensor_tensor(out=ot[b][:, 512:1024], in0=ot[b][:, 512:1024],
                                                          in1=bufB[:, t, :], op=ALU.add), deps=[ty])
                    SP.dma(out_d[t * 128:(t + 1) * 128, 0:512], ot[b][:, 0:512], d_o[b], deps=[xa])
                    od = SP.dma(out_d[t * 128:(t + 1) * 128, 512:1024], ot[b][:, 512:1024], d_o[b], deps=[xb])
                    o_free[b] = od
                    out_deps.append(od)
            SP.wait(out_deps)
            barrier()
    return nc


def _host_consts():
    invf64 = 500000.0 ** (-np.arange(0, 16, 2, dtype=np.float64) / 16.0)
    invf_hi = invf64.astype(np.float32)
    invf_lo = (invf64 - invf_hi.astype(np.float64)).astype(np.float32)
    invf = np.concatenate([invf_hi, invf_lo])
    invf = np.ascontiguousarray(np.broadcast_to(invf[None, :], (128, 16))).astype(np.float32)
    ident = np.eye(128, dtype=np.float32)
    c = np.arange(128)[:, None, None]
    d = np.arange(NT)[None, :, None]
    r = np.arange(128)[None, None, :]
    delta = 128 * d + r - c
    cnt = ((delta >= 0) & (delta <= 128)).astype(np.float32)
    cnt += ((delta >= 0) & (delta % 4 == 0) & (delta <= 512)).astype(np.float32)
    cnt += ((delta >= 0) & (delta % 16 == 0) & (delta <= 2048)).astype(np.float32)
    maskc = np.ascontiguousarray(cnt.reshape(128, NT * 128)).astype(np.float32)
    j = np.arange(128)[:, None]
    i = np.arange(128)[None, :]
    tri = (j <= i).astype(np.float32)
    return invf, ident, maskc, tri


_NC_CACHE = {}


def kernel(x, positions, pre_mix_norm, w_in, sgu_ln_gain, sgu_ln_bias, sgu_w_spatial, sgu_b_spatial,
           attn_out_norm, sgu_out_norm, w_out, post_mix_norm, pre_ffn_norm, w_gate, w_up, w_down,
           post_ffn_norm):
    f32 = np.float32
    x = np.asarray(x, f32)
    positions = np.asarray(positions, np.int32)
    B = x.shape[0]
    invf, ident, maskc, tri = _host_consts()
    w_gate = np.asarray(w_gate, f32)[0]
    w_up = np.asarray(w_up, f32)[0]
    wg = w_gate.reshape(8, 128, NFF, 128).transpose(2, 1, 0, 3)
    wu = w_up.reshape(8, 128, NFF, 128).transpose(2, 1, 0, 3)
    w_gu = np.ascontiguousarray(np.stack([wg, wu], axis=2)).reshape(NFF, 128, 2 * 8 * 128)
    wsT = np.ascontiguousarray(np.asarray(sgu_w_spatial, f32)[0].transpose(2, 0, 1)).reshape(128, 8 * 128)
    bsp = np.ascontiguousarray(np.asarray(sgu_b_spatial, f32)[0].T)
    g_mix = np.concatenate([np.asarray(attn_out_norm, f32)[0], np.asarray(sgu_out_norm, f32)[0]])[None, :]
    def rep(v):
        return np.ascontiguousarray(np.broadcast_to(v[None, :], (128, v.shape[0]))).astype(f32)

    shared = dict(
        invf=invf, ident=ident, maskc=maskc, tri=tri,
        g_premix=rep(np.asarray(pre_mix_norm, f32)[0]),
        g_postmix=rep(np.asarray(post_mix_norm, f32)[0]),
        g_preffn=rep(np.asarray(pre_ffn_norm, f32)[0]),
        g_postffn=rep(np.asarray(post_ffn_norm, f32)[0]),
        g_mixT=np.ascontiguousarray(g_mix[0].reshape(8, 128).T),
        ln_g=rep(np.asarray(sgu_ln_gain, f32)[0]),
        ln_b=rep(np.asarray(sgu_ln_bias, f32)[0]),
        w_in=np.ascontiguousarray(np.asarray(w_in, f32)[0]),
        wsT=wsT, bsp=bsp,
        w_out=np.ascontiguousarray(np.asarray(w_out, f32)[0]),
        w_gu=w_gu,
        w_down=np.ascontiguousarray(np.asarray(w_down, f32)[0]),
    )
    in_maps = []
    for b in range(B):
        m = dict(shared)
        m["x"] = np.ascontiguousarray(x[b])
        m["pos"] = np.ascontiguousarray(positions[b].reshape(NT, 128).T)
        in_maps.append(m)
    if "nc" not in _NC_CACHE:
        _NC_CACHE["nc"] = build_program()
    nc = _NC_CACHE["nc"]
    res = run_bass_kernel_spmd(nc, in_maps, core_ids=list(range(B)))
    _NC_CACHE["last"] = res
    out = np.stack([np.asarray(r["out"], f32) for r in res.results], axis=0)
    return out
```

```python
import numpy as np
from contextlib import ExitStack
import concourse.bass as bass
import concourse.mybir as mybir
from concourse.bass_utils import run_bass_kernel_spmd

F32 = mybir.dt.float32
BF16 = mybir.dt.bfloat16
I32 = mybir.dt.int32
AF = mybir.ActivationFunctionType
ALU = mybir.AluOpType

S = 2048
D = 1024
NT = 16
DFF = 2816
NFF = 22
INW = 2560
RMS_EPS = 1e-6
LN_EPS = 1e-5
TWO_PI = 2.0 * np.pi
C1 = 6.28125
C2 = TWO_PI - 6.28125
PI_LO = 3.1415925

DEBUG = {}
STOP = None
VERBOSE = False
TSTAGE = 0


class Sem:
    def __init__(self, es, nc, name):
        self.sem = es.enter_context(nc.semaphore(name))
        self.cnt = 0
        self.key = name


class Eng:
    def __init__(self, es, nc, eng, name):
        self.e = eng
        self.s = Sem(es, nc, "s_" + name)
        self.seen = {}
        self.name = name

    def wait(self, deps):
        for d in deps:
            if d is None:
                continue
            s, c = d
            if self.seen.get(s.key, 0) >= c:
                continue
            self.e.wait_ge(s.sem, c)
            self.seen[s.key] = c

    def op(self, fn, deps=(), signal=True):
        self.wait(deps)
        ins = fn(self.e)
        if signal:
            self.s.cnt += 1
            ins.then_inc(self.s.sem, 1)
            return (self.s, self.s.cnt)
        return None

    def dma(self, out, in_, dsem, deps=()):
        self.wait(deps)
        ins = self.e.dma_start(out=out, in_=in_)
        dsem.cnt += 16
        ins.then_inc(dsem.sem, 16)
        return (dsem, dsem.cnt)

    def last(self):
        return (self.s, self.s.cnt) if self.s.cnt else None


def build_program():
    nc = bass.Bass("TRN2", target_bir_lowering=False)

    def din(name, shape, dt=F32):
        return nc.dram_tensor(name, list(shape), dt, kind="ExternalInput").ap()

    x_d = din("x", [S, D])
    pos_d = din("pos", [128, NT], I32)
    invf_d = din("invf", [128, 16])
    ident_d = din("ident", [128, 128])
    maskc_d = din("maskc", [128, NT * 128])
    tri_d = din("tri", [128, 128])
    g_premix_d = din("g_premix", [128, D])
    g_postmix_d = din("g_postmix", [128, D])
    g_preffn_d = din("g_preffn", [128, D])
    g_postffn_d = din("g_postffn", [128, D])
    g_mixT_d = din("g_mixT", [128, 8])
    ln_g_d = din("ln_g", [128, 512])
    ln_b_d = din("ln_b", [128, 512])
    w_in_d = din("w_in", [D, INW])
    wsT_d = din("wsT", [128, 8 * 128])
    bsp_d = din("bsp", [128, 8])
    w_out_d = din("w_out", [D, D])
    w_gu_d = din("w_gu", [NFF, 128, 2 * 8 * 128])
    w_down_d = din("w_down", [DFF, D])
    out_d = nc.dram_tensor("out", [S, D], F32, kind="ExternalOutput").ap()
    dbg_d = {}
    for k, (shape, dt) in DEBUG.items():
        dbg_d[k] = nc.dram_tensor("dbg_" + k, list(shape), dt, kind="ExternalOutput").ap()

    w_in_v = w_in_d.rearrange("(kc p) n -> p kc n", p=128)
    w_out_v = w_out_d.rearrange("(kc p) n -> p kc n", p=128)

    with ExitStack() as es0:
        SP = Eng(es0, nc, nc.sync, "sp")
        ACT = Eng(es0, nc, nc.scalar, "act")
        DVE = Eng(es0, nc, nc.vector, "dve")
        PE = Eng(es0, nc, nc.tensor, "pe")
        POOL = Eng(es0, nc, nc.gpsimd, "pool")
        ENGS = [SP, ACT, DVE, PE, POOL]
        dma_sems = []

        def dsem(name):
            s = Sem(es0, nc, "d_" + name)
            dma_sems.append(s)
            return s

        def barrier():
            lasts = [e.last() for e in ENGS] + [(s, s.cnt) for s in dma_sems if s.cnt]
            for e in ENGS:
                e.wait(lasts)

        uid = [0]

        def sb(es, name, shape, dt):
            uid[0] += 1
            return es.enter_context(nc.sbuf_tensor(f"sb_{name}_{uid[0]}", list(shape), dt))

        def ps(es, name, shape, dt=F32):
            uid[0] += 1
            return es.enter_context(nc.psum_tensor(f"pp_{name}_{uid[0]}", list(shape), dt))

        class Carver:
            def __init__(self, flat):
                self.flat = flat
                self.off = 0

            def get(self, shape, dt):
                n = int(np.prod(shape[1:]))
                nf = (n * (4 if dt in (F32, I32) else 2) + 3) // 4
                v = self.flat[:, self.off:self.off + nf]
                self.off += nf
                assert self.off <= self.flat.shape[1]
                if dt != F32:
                    v = v.bitcast(dt)
                if len(shape) == 3:
                    v = v.rearrange("p (a b) -> p a b", b=shape[2])
                elif len(shape) == 4:
                    v = v.rearrange("p (a b c) -> p a b c", b=shape[2], c=shape[3])
                return v

        def dbg_dump(name, src_ap):
            if name in dbg_d:
                barrier()
                s = dsem("dbg_" + name)
                d = SP.dma(dbg_d[name], src_ap, s)
                SP.wait([d])

        bufA = sb(es0, "bufA", [128, NT, 512], F32)
        bufB = sb(es0, "bufB", [128, NT, 512], F32)
        identb = sb(es0, "identb", [128, 128], BF16)
        epsr = sb(es0, "epsr", [128, 1], F32)
        epsl = sb(es0, "epsl", [128, 1], F32)
        halfpi = sb(es0, "halfpi", [128, 1], F32)
        junk = sb(es0, "junk", [128, 1024], BF16)
        stat = sb(es0, "stat", [128, 16 * NT], F32)
        xt = [sb(es0, f"xt{i}", [128, D], F32) for i in range(2)]
        d_xt = [dsem(f"xt{i}") for i in range(2)]
        d_c = dsem("consts")
        d_c2 = dsem("consts2")

        c_deps = []
        c_deps.append(POOL.dma(identb[:], ident_d, d_c))
        i0 = DVE.op(lambda e: e.memset(epsr[:], RMS_EPS))
        i0 = DVE.op(lambda e: e.memset(epsl[:], LN_EPS))
        i0 = DVE.op(lambda e: e.memset(halfpi[:], np.pi / 2))
        setup_dep = i0

        def ST(col, t):
            return stat[:, col * NT + t: col * NT + t + 1]

        h2T = sb(es0, "h2T", [128, 8, 1024], BF16)
        gB3 = sb(es0, "gB3", [128, D], F32)
        d_g3 = dsem("g3")
        g3_dep = SP.dma(gB3[:], g_preffn_d, d_g3)
        d1s = {"h2n_free": [None, None], "ps_free": None, "cnt": 0}

        def d1_front(t, h2n, in_deps):
            b = d1s["cnt"] % 2
            d1s["cnt"] += 1
            s1 = ACT.op(lambda e: e.activation(out=h2n[b][:, 0:512], in_=bufA[:, t, :], func=AF.Square,
                                               accum_out=ST(0, t)), deps=list(in_deps) + [d1s["h2n_free"][b]])
            s2 = ACT.op(lambda e: e.activation(out=h2n[b][:, 512:1024], in_=bufB[:, t, :],
                                               func=AF.Square, accum_out=ST(1, t)))
            ad = DVE.op(lambda e: e.tensor_tensor(out=ST(2, t), in0=ST(0, t), in1=ST(1, t),
                                                  op=ALU.add), deps=[s1, s2])
            q1 = ACT.op(lambda e: e.activation(out=ST(3, t), in_=ST(2, t), func=AF.Sqrt,
                                               scale=1.0 / D, bias=epsr[:]), deps=[ad])
            r1 = DVE.op(lambda e: e.reciprocal(out=ST(4, t), in_=ST(3, t)), deps=[q1])
            m1 = DVE.op(lambda e: e.scalar_tensor_tensor(
                out=h2n[b][:, 0:512], in0=bufA[:, t, :], scalar=ST(4, t), in1=gB3[:, 0:512],
                op0=ALU.mult, op1=ALU.mult), deps=[r1, d1s["h2n_free"][b], g3_dep])
            m2 = DVE.op(lambda e: e.scalar_tensor_tensor(
                out=h2n[b][:, 512:1024], in0=bufB[:, t, :], scalar=ST(4, t), in1=gB3[:, 512:1024],
                op0=ALU.mult, op1=ALU.mult), deps=[r1])
            return (b, [m1, m2])

        def d1_pe(fr, tl, h2n, psv, pe_deps):
            b, mdeps = fr
            PE.wait(mdeps + [d1s["ps_free"]] + list(pe_deps))
            for kc in range(8):
                tr = PE.op(lambda e: e.transpose(out=psv[:, kc, :],
                                                 in_=h2n[b][:, kc * 128:(kc + 1) * 128],
                                                 identity=identb[:]), signal=(kc == 7))
            d1s["h2n_free"][b] = tr
            ev = ACT.op(lambda e: e.activation(out=h2T[:, :, tl * 128:(tl + 1) * 128], in_=psv, func=AF.Copy),
                        deps=[tr])
            d1s["ps_free"] = ev
            return ev

        es1 = es0.enter_context(ExitStack())
        if True:
            vn = sb(es1, "vn", [128, NT, 512], BF16)
            with ExitStack() as es2:
                qkT = sb(es2, "qkT", [128, 8, S], BF16)
                vaug = sb(es2, "vaug", [128, NT, 8, 65], BF16)
                with ExitStack() as es3:
                    w_in = sb(es3, "w_in", [128, 8, INW], BF16)
                    cv = Carver(bufA[:].rearrange("p a b -> p (a b)"))
                    gB1 = cv.get([128, D], F32)
                    lngB = cv.get([128, 512], F32)
                    lnbB = cv.get([128, 512], F32)
                    posi = sb(es3, "posi", [128, NT], I32)
                    posf = sb(es3, "posf", [128, NT], F32)
                    invf = sb(es3, "invf", [128, 16], F32)
                    ang = sb(es3, "ang", [128, NT, 8], F32)
                    kf = sb(es3, "kf", [128, NT, 8], F32)
                    ki = sb(es3, "ki", [128, NT, 8], I32)
                    rr = sb(es3, "rr", [128, NT, 8], F32)
                    ra = sb(es3, "ra", [128, NT, 8], F32)
                    CC = sb(es3, "CC", [128, NT, 2, 8], F32)
                    SS = sb(es3, "SS", [128, NT, 2, 8], F32)
                    xn = [cv.get([128, D], BF16) for i in range(2)]
                    hT = [cv.get([128, 8, 128], BF16) for i in range(2)]
                    qkb = [cv.get([128, 16, 64], BF16) for i in range(2)]
                    T1 = cv.get([128, 16, 16], F32)
                    T2 = cv.get([128, 16, 16], F32)
                    vg = [cv.get([128, 512], F32) for i in range(2)]
                    tmpv = cv.get([128, 512], F32)
                    ps_t = [ps(es3, f"ps_t{i}", [128, 8, 128], BF16) for i in range(1)]
                    ps_tq = [ps(es3, f"ps_tq{i}", [128, 8, 128], BF16) for i in range(1)]
                    ps_qk = ps(es3, "ps_qk", [128, 1024])
                    ps_v = ps(es3, "ps_v", [128, 512])
                    ps_u = ps(es3, "ps_u", [128, 512])
                    ps_vs = ps(es3, "ps_vs", [128, 512])
                    d_w = [dsem(f"win{i}") for i in range(5)]

                    d_cg = dsem("cg1")
                    d_cl = dsem("cln")
                    g1_dep = SP.dma(gB1[:], g_premix_d, d_cg)
                    pos_deps = []
                    ln_deps = []
                    id_dep = c_deps[0]

                    def late_consts():
                        SP.dma(posi[:], pos_d, d_c2)
                        SP.dma(invf[:], invf_d, d_c2)
                        pos_deps.append((d_c2, d_c2.cnt))
                        SP.dma(lngB[:], ln_g_d, d_cl)
                        SP.dma(lnbB[:], ln_b_d, d_cl)
                        ln_deps.append((d_cl, d_cl.cnt))
                    w_deps = []
                    for cg in range(5):
                        w_deps.append(POOL.dma(w_in[:, :, cg * 512:(cg + 1) * 512],
                                               w_in_v[:, :, cg * 512:(cg + 1) * 512], d_w[cg]))
                    ones_dep = DVE.op(lambda e: e.memset(vaug[:, :, :, 64:65], 1.0))

                    tabs = {}

                    def make_tables():
                        a = DVE.op(lambda e: e.tensor_copy(out=posf[:], in_=posi[:]), deps=pos_deps)
                        a = DVE.op(lambda e: e.tensor_tensor(
                            out=ang[:], in0=posf[:].unsqueeze(2).to_broadcast([128, NT, 8]),
                            in1=invf[:, 0:8].unsqueeze(1).to_broadcast([128, NT, 8]), op=ALU.mult), deps=[a])
                        a2 = DVE.op(lambda e: e.tensor_tensor(
                            out=kf[:], in0=posf[:].unsqueeze(2).to_broadcast([128, NT, 8]),
                            in1=invf[:, 8:16].unsqueeze(1).to_broadcast([128, NT, 8]), op=ALU.mult), deps=[a])
                        a = DVE.op(lambda e: e.tensor_tensor(out=ang[:], in0=ang[:], in1=kf[:], op=ALU.add),
                                   deps=[a, a2])
                        a = DVE.op(lambda e: e.tensor_scalar(out=kf[:], in0=ang[:], scalar1=1.0 / TWO_PI,
                                                             scalar2=None, op0=ALU.mult), deps=[a])
                        a = DVE.op(lambda e: e.tensor_copy(out=ki[:], in_=kf[:]), deps=[a])
                        a = DVE.op(lambda e: e.tensor_copy(out=kf[:], in_=ki[:]), deps=[a])
                        a = DVE.op(lambda e: e.scalar_tensor_tensor(out=rr[:], in0=kf[:], scalar=-C1, in1=ang[:],
                                                                    op0=ALU.mult, op1=ALU.add), deps=[a])
                        a = DVE.op(lambda e: e.scalar_tensor_tensor(out=ra[:], in0=kf[:], scalar=-C2, in1=rr[:],
                                                                    op0=ALU.mult, op1=ALU.add), deps=[a])
                        a = DVE.op(lambda e: e.tensor_scalar(out=rr[:], in0=ra[:], scalar1=-PI_LO, scalar2=PI_LO,
                                                             op0=ALU.max, op1=ALU.min), deps=[a])
                        r_dep = a
                        a = DVE.op(lambda e: e.scalar_tensor_tensor(out=ra[:], in0=rr[:], scalar=-1.0, in1=rr[:],
                                                                    op0=ALU.mult, op1=ALU.max), deps=[a])
                        ra_dep = a
                        for j in range(2):
                            tab_dep = ACT.op(lambda e: e.activation(out=SS[:, :, j, :], in_=rr[:], func=AF.Sin),
                                             deps=[r_dep])
                        for j in range(2):
                            tab_dep = ACT.op(lambda e: e.activation(out=CC[:, :, j, :], in_=ra[:], func=AF.Sin,
                                                                    scale=-1.0, bias=halfpi[:]),
                                             deps=[ra_dep, setup_dep])

                        tabs['dep'] = tab_dep

                    if STOP == "S":
                        barrier()
                        return nc
                    if VERBOSE:
                        print('phaseA sbuf remaining', nc.sbuf_bytes_remaining)
                    NTA = NT
                    xt_free = [None, None]
                    xn_free = [None, None]
                    hT_free = [None, None]
                    qkb_free = [None, None]
                    vg_free = [None, None]
                    st = {"pst_free": None, "pstq_free": None, "ln_pending": None, "T_free": None}
                    acc_free = {"qk": [], "v": [], "u": [], "vs": []}
                    ld_dep = {}
                    xn_dep = {}
                    hT_dep = {}

                    def ln_sqrt(tp, bp, sq_dep):
                        return ACT.op(lambda e: e.activation(out=ST(6, tp), in_=ST(5, tp), func=AF.Sqrt,
                                                             scale=1.0 / 512, bias=epsl[:]), deps=[sq_dep, setup_dep])

                    def ln_dve(tp, bp, s1):
                        r1 = DVE.op(lambda e: e.reciprocal(out=ST(7, tp), in_=ST(6, tp)), deps=[s1])
                        t1 = DVE.op(lambda e: e.scalar_tensor_tensor(
                            out=tmpv[:], in0=vg[bp][:], scalar=ST(4, tp), in1=lngB[:],
                            op0=ALU.add, op1=ALU.mult), deps=[r1] + ln_deps)
                        t2 = DVE.op(lambda e: e.scalar_tensor_tensor(
                            out=vn[:, tp, :], in0=tmpv[:], scalar=ST(7, tp), in1=lnbB[:],
                            op0=ALU.mult, op1=ALU.add), deps=[t1])
                        return t2

                    def ln_finish(tp, bp, sq_dep):
                        return ln_dve(tp, bp, ln_sqrt(tp, bp, sq_dep))

                    sqj = sb(es3, "sqj", [128, D], BF16)
                    xt3 = [xt[0], xt[1], cv.get([128, D], F32)]
                    d_xt3 = d_xt + [dsem("xt2")]
                    xt_free = [None, None, None]

                    def load_x(t):
                        b3 = t % 3
                        ld_dep[t] = SP.dma(xt3[b3][:], x_d[t * 128:(t + 1) * 128, :], d_xt3[b3],
                                           deps=[xt_free[b3], xn_dep.get(t - 3)])

                    rs_dep = {}

                    def front_stats(t):
                        b = t % 2
                        b3 = t % 3
                        ld = ld_dep[t]
                        xnd = DVE.op(lambda e: e.tensor_tensor(out=xn[b][:], in0=xt3[b3][:], in1=gB1[:],
                                                               op=ALU.mult), deps=[ld, xn_free[b], g1_dep])
                        xn_dep[t] = xnd
                        sq = ACT.op(lambda e: e.activation(out=sqj[:], in_=xt3[b3][:], func=AF.Square,
                                                           accum_out=ST(0, t)), deps=[ld])
                        xt_free[b3] = sq
                        sd = ACT.op(lambda e: e.activation(out=ST(1, t), in_=ST(0, t), func=AF.Sqrt,
                                                           scale=1.0 / D, bias=epsr[:]), deps=[sq, setup_dep])
                        lp = st["ln_pending"]
                        s1 = None
                        if lp is not None:
                            s1 = ln_sqrt(*lp)
                        rs_dep[t] = DVE.op(lambda e: e.reciprocal(out=ST(2, t), in_=ST(1, t)), deps=[sd])
                        if lp is not None:
                            vg_free[lp[1]] = ln_dve(lp[0], lp[1], s1)
                            st["ln_pending"] = None

                    def front_pe(t):
                        b = t % 2
                        PE.wait([xn_dep[t], st["pst_free"], id_dep])
                        for kc in range(8):
                            tr = PE.op(lambda e: e.transpose(out=ps_t[0][:, kc, :],
                                                             in_=xn[b][:, kc * 128:(kc + 1) * 128],
                                                             identity=identb[:]), signal=(kc == 7))
                        xn_free[b] = tr
                        ev = ACT.op(lambda e: e.activation(out=hT[b][:], in_=ps_t[0][:], func=AF.Copy),
                                    deps=[tr, hT_free[b]])
                        st["pst_free"] = ev
                        hT_dep[t] = ev

                    banks = [(ps_qk[:, 0:512], "qk"), (ps_qk[:, 512:1024], "qk"), (ps_v[:], "v"),
                             (ps_u[:], "u"), (ps_vs[:], "vs")]
                    mm_deps = {}

                    def mm(t, cgs):
                        b = t % 2
                        for cg in cgs:
                            ap_out, key = banks[cg]
                            PE.wait([hT_dep[t], w_deps[cg]] + acc_free[key])
                            for kc in range(8):
                                m = PE.op(lambda e: e.matmul(out=ap_out, lhsT=hT[b][:, kc, :],
                                                             rhs=w_in[:, kc, cg * 512:(cg + 1) * 512],
                                                             start=(kc == 0), stop=(kc == 7)),
                                          signal=(kc == 7))
                            mm_deps[(t, cg)] = m
                        if 4 in cgs:
                            hT_free[b] = mm_deps[(t, 4)]

                    qk_ready = {}
                    junk_free = [None, None]

                    def back_qkv(t):
                        b = t % 2
                        qkv = ps_qk[:].rearrange("p (h d) -> p h d", d=64)
                        md = mm_deps[(t, 1)]
                        t1 = DVE.op(lambda e: e.scalar_tensor_tensor(
                            out=T1[:], in0=qkv[:, :, 0:16], scalar=ST(2, t),
                            in1=CC[:, t, :, :].rearrange("p a b -> p (a b)").unsqueeze(1).to_broadcast([128, 16, 16]),
                            op0=ALU.mult, op1=ALU.mult), deps=[md, tabs['dep'], st["T_free"], rs_dep[t]])
                        t2 = DVE.op(lambda e: e.scalar_tensor_tensor(
                            out=T2[:], in0=qkv[:, :, 0:16], scalar=ST(2, t),
                            in1=SS[:, t, :, :].rearrange("p a b -> p (a b)").unsqueeze(1).to_broadcast([128, 16, 16]),
                            op0=ALU.mult, op1=ALU.mult), deps=[md, tabs['dep']])
                        cp = DVE.op(lambda e: e.tensor_scalar(out=qkb[b][:, :, 16:64], in0=qkv[:, :, 16:64],
                                                              scalar1=ST(2, t), scalar2=None, op0=ALU.mult),
                                    deps=[md, qkb_free[b], t1, t2, rs_dep[t]])
                        o1 = DVE.op(lambda e: e.tensor_tensor(out=qkb[b][:, :, 0:8], in0=T1[:, :, 0:8],
                                                              in1=T2[:, :, 8:16], op=ALU.subtract),
                                    deps=[t1, t2, qkb_free[b]])
                        o2 = DVE.op(lambda e: e.tensor_tensor(out=qkb[b][:, :, 8:16], in0=T1[:, :, 8:16],
                                                              in1=T2[:, :, 0:8], op=ALU.add),
                                    deps=[t1, t2])
                        st["T_free"] = o2
                        acc_free["qk"] = [cp, t1, t2]
                        vcp = DVE.op(lambda e: e.tensor_scalar(
                            out=vaug[:, t, :, 0:64], in0=ps_v[:].rearrange("p (h d) -> p h d", d=64),
                            scalar1=ST(2, t), scalar2=None, op0=ALU.mult),
                            deps=[mm_deps[(t, 2)], ones_dep, rs_dep[t]])
                        acc_free["v"] = [vcp]
                        qk_ready[t] = [cp, o1, o2]

                    def back_rest(t):
                        b = t % 2
                        gu_ = ACT.op(lambda e: e.activation(out=bufB[:, t, :], in_=ps_u[:], func=AF.Gelu,
                                                            scale=ST(2, t)), deps=[mm_deps[(t, 3)], rs_dep[t]])
                        acc_free["u"] = [gu_]
                        gv = ACT.op(lambda e: e.activation(out=vg[b][:], in_=ps_vs[:], func=AF.Gelu, scale=ST(2, t),
                                                           accum_out=ST(3, t)), deps=[mm_deps[(t, 4)], vg_free[b]])
                        acc_free["vs"] = [gv]
                        nm = DVE.op(lambda e: e.tensor_scalar(out=ST(4, t), in0=ST(3, t), scalar1=-1.0 / 512,
                                                              scalar2=None, op0=ALU.mult), deps=[gv])
                        sq2 = ACT.op(lambda e: e.activation(out=junk[:, b * 512:(b + 1) * 512], in_=vg[b][:],
                                                            func=AF.Square, bias=ST(4, t), accum_out=ST(5, t)),
                                     deps=[nm, junk_free[b]])
                        junk_free[b] = sq2
                        st["ln_pending"] = (t, b, sq2)
                        PE.wait(qk_ready[t] + [st["pstq_free"]])
                        for c in range(8):
                            trq = PE.op(lambda e: e.transpose(
                                out=ps_tq[0][:, c, :],
                                in_=qkb[b][:, 2 * c:2 * c + 2, :].rearrange("p a b -> p (a b)"),
                                identity=identb[:]), signal=(c == 7))
                        qkb_free[b] = trq
                        evq = DVE.op(lambda e: e.tensor_copy(out=qkT[:, :, t * 128:(t + 1) * 128],
                                                             in_=ps_tq[0][:]), deps=[trq])
                        st["pstq_free"] = evq

                    load_x(0)
                    load_x(1)
                    load_x(2)
                    late_consts()
                    front_stats(0)
                    front_pe(0)
                    front_stats(1)
                    make_tables()
                    for t in range(NTA):
                        mm(t, [0, 1, 2])
                        if t + 2 < NTA:
                            front_stats(t + 2)
                        elif st["ln_pending"] is not None:
                            lp = st["ln_pending"]
                            vg_free[lp[1]] = ln_finish(*lp)
                            st["ln_pending"] = None
                        back_qkv(t)
                        if t + 1 < NTA:
                            front_pe(t + 1)
                        if t + 3 < NTA:
                            load_x(t + 3)
                        mm(t, [3, 4])
                        back_rest(t)
                    maskb = xt[1][:].bitcast(BF16).rearrange("p (a b) -> p a b", b=128)
                    d_m = dsem("mask")
                    mdep = POOL.dma(xt[1][:].bitcast(BF16), maskc_d, d_m,
                                    deps=[xt_free[1], xn_dep[13], xt_free[0], xt_free[2]])
                    ln_pending = st["ln_pending"]
                    barrier()
                ln_last = ln_finish(*ln_pending)
                if STOP == "A0":
                    return nc
                dbg_dump("qkT", qkT[:].rearrange("p a b -> p (a b)"))
                dbg_dump("vaug", vaug[:].rearrange("p a b c -> p (a b c)"))
                dbg_dump("vn", vn[:].rearrange("p a b -> p (a b)"))
                dbg_dump("gu", bufB[:].rearrange("p a b -> p (a b)"))

                if STOP == "A":
                    return nc
                with ExitStack() as es3:
                    pT = [sb(es3, f"pT{i}", [128, 2, 512], BF16) for i in range(5)]
                    rec = sb(es3, "rec", [128, 2, 4], F32)
                    ps_s = [ps(es3, f"ps_s{i}", [128, 2, 512]) for i in range(3)]
                    acc = ps(es3, "acc", [128, 2, 512])
                    if VERBOSE:
                        print('phaseB sbuf remaining', nc.sbuf_bytes_remaining)
                    wsT_f = xt[0][:].rearrange("p (a b) -> p a b", b=128)
                    junkf = junk[:].bitcast(F32)
                    trib_f = junkf[:, 0:128]
                    bsp = junkf[:, 128:136]
                    gmT = junkf[:, 136:144]
                    d_c3 = dsem("c3")
                    SP.dma(xt[0][:], wsT_d, d_c3)
                    SP.dma(trib_f, tri_d, d_c3)
                    SP.dma(bsp, bsp_d, d_c3)
                    SP.dma(gmT, g_mixT_d, d_c3)
                    cw = [(d_c3, d_c3.cnt)]
                    stb = {"acc_ready": None}
                    pss_free = [None, None, None]
                    pT_free = [None, None, None, None, None]
                    steps = []
                    last_of_group = {}
                    for hp in range(4):
                        for G in range(4):
                            big = list(range(0, 4 * G))
                            small = list(range(4 * G, 4 * G + 4))
                            order = []
                            if big:
                                order.append(big.pop(0))
                            else:
                                order.append(small.pop(0))
                            while big or small:
                                if small:
                                    order.append(small.pop())
                                if big:
                                    order.append(big.pop(0))
                                if big:
                                    order.append(big.pop(0))
                            for m in order:
                                steps.append((hp, G, m))
                            last_of_group[(hp, G)] = order[-1]
                    info = {}

                    def emit_pv(si):
                        hp, G, m = steps[si]
                        slot, lst, mk_dep = info[si]
                        PE.wait([mk_dep, stb["acc_ready"]] if m == 0 else [mk_dep])
                        last = None
                        for idx, (hh, j, jq, m_) in enumerate(lst):
                            first = (m_ == 0 and j == 0)
                            last = PE.op(lambda e: e.matmul(
                                out=acc[:, hh % 2, jq * 65:(jq + 1) * 65],
                                lhsT=pT[slot][:, hh % 2, j * 128:(j + 1) * 128],
                                rhs=vaug[:, m_, hh, :], start=first, stop=False, skip_group_check=True),
                                signal=(idx == len(lst) - 1))
                        pT_free[slot] = last
                        if m == last_of_group[(hp, G)]:
                            accv = acc[:, :, 0:260].rearrange("p h (q c) -> p h q c", c=65)
                            rc = DVE.op(lambda e: e.reciprocal(out=rec[:], in_=accv[:, :, :, 64]),
                                        deps=[last, ln_last])
                            fo = DVE.op(lambda e: e.tensor_tensor(
                                out=bufA[:, 4 * G:4 * G + 4, hp * 128:(hp + 1) * 128]
                                .rearrange("p q (h c) -> p q h c", c=64),
                                in0=accv.rearrange("p h q c -> p q h c")[:, :, :, 0:64],
                                in1=rec[:].rearrange("p h q -> p q h").unsqueeze(3).to_broadcast([128, 4, 2, 64]),
                                op=ALU.mult), deps=[rc])
                            stb["acc_ready"] = fo

                    LAG = 3
                    for si, (hp, G, m) in enumerate(steps):
                        n0 = max(m, 4 * G)
                        cnt = 4 * G + 4 - n0
                        N = 128 * cnt
                        d0 = n0 - m
                        sl = si % 3
                        pl = si % 5
                        PE.wait([pss_free[sl]])
                        for hh in range(2):
                            hb = 64 * hh
                            qk = PE.op(lambda e: e.matmul(
                                out=ps_s[sl][:, hh, 0:N],
                                lhsT=qkT[hb:hb + 64, 4 + hp, m * 128:(m + 1) * 128],
                                rhs=qkT[hb:hb + 64, hp, n0 * 128:(4 * G + 4) * 128],
                                start=True, stop=True), signal=(hh == 1))
                        if si >= LAG:
                            emit_pv(si - LAG)
                        ex = ACT.op(lambda e: e.activation(out=pT[pl][:, :, 0:N], in_=ps_s[sl][:, :, 0:N],
                                                           func=AF.Exp, scale=0.125),
                                    deps=[qk, pT_free[pl]])
                        pss_free[sl] = ex
                        mk = DVE.op(lambda e: e.tensor_tensor(
                            out=pT[pl][:, :, 0:N], in0=pT[pl][:, :, 0:N],
                            in1=maskb[:, d0:d0 + cnt, :].rearrange("p a b -> p (a b)").unsqueeze(1)
                            .to_broadcast([128, 2, N]), op=ALU.mult), deps=[ex, mdep])
                        lst = []
                        for hh in range(2):
                            for j in range(cnt):
                                lst.append((2 * hp + hh, j, n0 + j - 4 * G, m))
                        info[si] = (pl, lst, mk)
                    for si in range(len(steps) - LAG, len(steps)):
                        emit_pv(si)
                    barrier()
                dbg_dump("attn", bufA[:].rearrange("p a b -> p (a b)"))
                if STOP == "B":
                    return nc
            with ExitStack() as es2:
                w_out = sb(es2, "w_out", [128, 8, D], BF16)
                wsT = sb(es2, "wsT", [128, 8, 128], BF16)
                trib = sb(es2, "trib", [128, 128], BF16)
                gB2 = sb(es2, "gB2", [128, D], F32)
                mixb = [sb(es2, f"mixb{i}", [128, D], BF16) for i in range(2)]
                mT = [sb(es2, f"mT{i}", [128, 8, 128], BF16) for i in range(2)]
                tmpy = [sb(es2, f"tmpy{i}", [128, D], F32) for i in range(2)]
                h2n_c = [sb(es2, f"h2nc{i}", [128, D], BF16) for i in range(2)]
                ps_sp = [ps(es2, f"ps_sp{i}", [128, 8, 64]) for i in range(2)]
                ps_tm = [ps(es2, f"ps_tm{i}", [128, 8, 128], BF16) for i in range(2)]
                ps_y = [ps(es2, f"ps_y{i}", [128, D]) for i in range(2)]
                if VERBOSE:
                    print('phaseC sbuf remaining', nc.sbuf_bytes_remaining)
                d_c4 = dsem("c4")
                d_c3b = dsem("c3b")
                bsp_dep = cw
                SP.dma(gB2[:], g_postmix_d, d_c4)
                cs = [(d_c4, d_c4.cnt)] + cw
                cwo = [POOL.dma(w_out[:], w_out_v, d_c3b)]
                wm = DVE.op(lambda e: e.tensor_tensor(
                    out=wsT[:], in0=wsT_f, in1=trib_f.unsqueeze(1).to_broadcast([128, 8, 128]),
                    op=ALU.mult), deps=cw)
                def stat_cols(col, t0, n):
                    return stat[:, col * NT + t0: col * NT + t0 + n]

                scr = [None, None]
                scr2 = [None, None]
                for t in range(NT):
                    sa = ACT.op(lambda e: e.activation(out=h2n_c[t % 2][:, 0:512], in_=bufA[:, t, :],
                                                       func=AF.Square, accum_out=ST(8, t)), deps=[scr[t % 2]])
                    scr[t % 2] = sa
                qa = ACT.op(lambda e: e.activation(out=stat_cols(10, 0, NT), in_=stat_cols(8, 0, NT), func=AF.Sqrt,
                                                   scale=1.0 / 512, bias=epsr[:]), deps=[sa])
                sp_free = [None, None]
                spd = None
                sp_half = [None, None]
                rs_half = [None, None]
                qs_half = [None, None]
                k = 0
                mixb_free = [None, None]
                mixd = {}

                def frontc_stats(t):
                    b = t % 2
                    m1 = ACT.op(lambda e: e.activation(out=mixb[b][:, 0:512], in_=bufA[:, t, :], func=AF.Copy,
                                                       scale=ST(12, t)), deps=[ra_dep, mixb_free[b]])
                    m2 = DVE.op(lambda e: e.tensor_scalar(out=mixb[b][:, 512:1024], in0=bufB[:, t, :],
                                                          scalar1=ST(13, t), scalar2=None, op0=ALU.mult),
                                deps=[rs_half[t // 8], mixb_free[b]])
                    mixd[t] = [m1, m2]

                mT_free = [None, None]
                pstm_free = [None, None]
                mTd = {}

                def frontc_pe(t):
                    b = t % 2
                    PE.wait(mixd[t] + [pstm_free[b]])
                    for kc in range(8):
                        tr = PE.op(lambda e: e.transpose(out=ps_tm[b][:, kc, :],
                                                         in_=mixb[b][:, kc * 128:(kc + 1) * 128],
                                                         identity=identb[:]), signal=(kc == 7))
                    mixb_free[b] = tr
                    ev = ACT.op(lambda e: e.activation(out=mT[b][:], in_=ps_tm[b][:], func=AF.Copy),
                                deps=[tr, mT_free[b]])
                    pstm_free[b] = ev
                    mTd[t] = ev

                for hf in range(2):
                    for g in range(8):
                        sl = k % 2
                        mm = PE.op(lambda e: e.matmul(out=ps_sp[sl][:], lhsT=wsT[:, g, :],
                                                      rhs=vn[:, hf * 8:(hf + 1) * 8, g * 64:(g + 1) * 64],
                                                      start=True, stop=True), deps=[wm, sp_free[sl]])
                        spd = DVE.op(lambda e: e.scalar_tensor_tensor(
                            out=bufB[:, hf * 8:(hf + 1) * 8, g * 64:(g + 1) * 64], in0=ps_sp[sl][:],
                            scalar=bsp[:, g:g + 1], in1=bufB[:, hf * 8:(hf + 1) * 8, g * 64:(g + 1) * 64],
                            op0=ALU.add, op1=ALU.mult), deps=[mm] + bsp_dep)
                        sp_free[sl] = spd
                        k += 1
                    sp_half[hf] = spd
                    if hf == 1:
                        ra_dep = DVE.op(lambda e: e.reciprocal(out=stat_cols(12, 0, NT), in_=stat_cols(10, 0, NT)),
                                        deps=[qa])
                        rs_half[0] = DVE.op(lambda e: e.reciprocal(out=stat_cols(13, 0, 8),
                                                                   in_=stat_cols(11, 0, 8)), deps=[qs_half[0]])
                        for t in range(2):
                            m1 = ACT.op(lambda e: e.activation(
                                out=mixb[t % 2][:, 0:512], in_=bufA[:, t, :], func=AF.Copy, scale=ST(12, t)),
                                deps=[ra_dep])
                            m2 = DVE.op(lambda e: e.tensor_scalar(out=mixb[t % 2][:, 512:1024], in0=bufB[:, t, :],
                                                                  scalar1=ST(13, t), scalar2=None, op0=ALU.mult),
                                        deps=[rs_half[0]])
                            mixd[t] = [m1, m2]
                        frontc_pe(0)
                    for t in range(hf * 8, hf * 8 + 8):
                        ss_ = ACT.op(lambda e: e.activation(out=h2n_c[t % 2][:, 512:1024], in_=bufB[:, t, :],
                                                            func=AF.Square, accum_out=ST(9, t)),
                                     deps=[spd, scr2[t % 2]])
                        scr2[t % 2] = ss_
                    qs_half[hf] = ACT.op(lambda e: e.activation(
                        out=stat_cols(11, hf * 8, 8), in_=stat_cols(9, hf * 8, 8),
                        func=AF.Sqrt, scale=1.0 / 512, bias=epsr[:]), deps=[ss_])
                sp_last_mm = mm
                for kc in range(8):
                    wsc = DVE.op(lambda e: e.tensor_scalar(out=w_out[:, kc, :], in0=w_out[:, kc, :],
                                                           scalar1=gmT[:, kc:kc + 1], scalar2=None, op0=ALU.mult),
                                 deps=cwo + cs)
                cwo = [wsc]
                rs_half[1] = DVE.op(lambda e: e.reciprocal(out=stat_cols(13, 8, 8), in_=stat_cols(11, 8, 8)),
                                    deps=[qs_half[1]])
                cvD = Carver(vn[:].rearrange("p a b -> p (a b)").bitcast(F32))
                gu = [cvD.get([128, 2, 8, 128], BF16) for i in range(2)]
                gB4 = cvD.get([128, D], F32)
                sg = [cvD.get([128, 512], F32) for i in range(2)]
                d_gu = [dsem(f"gu{i}") for i in range(2)]
                d_c5 = dsem("c5")
                ffn = {"gu_idx": 0, "gu_free": [sp_last_mm, sp_last_mm], "pending_w": []}

                def issue_gu(f):
                    sl = ffn["gu_idx"] % 2
                    dep = POOL.dma(gu[sl][:].rearrange("p a b c -> p (a b c)"), w_gu_d[f], d_gu[sl],
                                   deps=[ffn["gu_free"][sl]])
                    ffn["gu_idx"] += 1
                    return (sl, dep)

                ffn["pending_w"].append(issue_gu(0))
                csD = [SP.dma(gB4[:], g_postffn_d, d_c5, deps=[sp_last_mm])]
                dbg_dump("sgu", bufB[:].rearrange("p a b -> p (a b)"))
                psy_free = [None, None]
                xt_free = [None, None]
                stc = {"tmpy_free": None}
                ldc = {}
                mmd = {}

                def loadc(t):
                    b = t % 2
                    ldc[t] = SP.dma(xt[b][:], x_d[t * 128:(t + 1) * 128, :], d_xt[b], deps=[xt_free[b], wm])

                def mmc(t, cg):
                    b = t % 2
                    PE.wait([mTd[t], psy_free[b]] + cwo)
                    for kc in range(8):
                        mm_ = PE.op(lambda e: e.matmul(out=ps_y[b][:, cg * 512:(cg + 1) * 512],
                                                       lhsT=mT[b][:, kc, :],
                                                       rhs=w_out[:, kc, cg * 512:(cg + 1) * 512],
                                                       start=(kc == 0), stop=(kc == 7)),
                                    signal=(kc == 7))
                    mmd[(t, cg)] = mm_
                    if cg == 1:
                        mT_free[b] = mm_

                def backc(t):
                    b = t % 2
                    mm_ = mmd[(t, 1)]
                    sy = ACT.op(lambda e: e.activation(out=tmpy[b][:], in_=ps_y[b][:], func=AF.Square,
                                                       accum_out=ST(14, t)), deps=[mm_, tmpy_free[b]])
                    qy = ACT.op(lambda e: e.activation(out=ST(15, t), in_=ST(14, t), func=AF.Sqrt,
                                                       scale=1.0 / D, bias=epsr[:]), deps=[sy])
                    ry = DVE.op(lambda e: e.reciprocal(out=ST(14, t), in_=ST(15, t)), deps=[qy])
                    ty = DVE.op(lambda e: e.scalar_tensor_tensor(
                        out=tmpy[b][:], in0=ps_y[b][:], scalar=ST(14, t), in1=gB2[:],
                        op0=ALU.mult, op1=ALU.mult), deps=[ry, tmpy_free[b]])
                    psy_free[b] = ty
                    tyd[t] = ty
                    xa = DVE.op(lambda e: e.tensor_tensor(out=bufA[:, t, :], in0=tmpy[b][:, 0:512],
                                                          in1=xt[b][:, 0:512], op=ALU.add),
                                deps=[ty, ldc[t], mixb_free[b]])
                    xb = DVE.op(lambda e: e.tensor_tensor(out=bufB[:, t, :], in0=tmpy[b][:, 512:1024],
                                                          in1=xt[b][:, 512:1024], op=ALU.add), deps=[ty])
                    tmpy_free[b] = xb
                    xt_free[b] = xb
                    xdone[t] = xb

                tmpy_free = [None, None]
                xdone = {}
                tyd = {}
                psv_c = ps_sp[0][:].rearrange("p a b -> p (a b)").bitcast(BF16).rearrange("p (a b) -> p a b", b=128)
                loadc(0)
                loadc(1)
                frontc_stats(2)
                fr_pend = None
                for t in range(NT):
                    if t + 1 < NT:
                        frontc_pe(t + 1)
                    mmc(t, 0)
                    mmc(t, 1)
                    if t + 3 < NT:
                        frontc_stats(t + 3)
                    if fr_pend is not None:
                        d1_pe(fr_pend[0], fr_pend[1], h2n_c, psv_c, [spd])
                        fr_pend = None
                    jd1 = {1: 0, 3: 1, 5: 2, 7: 3, 9: 4, 11: 5, 12: 6, 13: 7}.get(t)
                    if jd1 is not None:
                        fr_pend = (d1_front(jd1, h2n_c, [xdone[jd1]]), jd1)
                    backc(t)
                    if t + 2 < NT:
                        loadc(t + 2)
                if fr_pend is not None:
                    d1_pe(fr_pend[0], fr_pend[1], h2n_c, psv_c, [spd])
                c_act_psum = (ACT.s, max(d1s["ps_free"][1], mTd[NT - 1][1], mTd[NT - 2][1]))
                c_end = [ACT.last(), DVE.last(), PE.last()]
                c_psf = [tyd[NT - 2], tyd[NT - 1]]
                c_xb = [xdone[NT - 2], xdone[NT - 1]]
        dbg_dump("x1a", bufA[:].rearrange("p a b -> p (a b)"))
        dbg_dump("x1b", bufB[:].rearrange("p a b -> p (a b)"))
        if STOP == "C":
            return nc
        with ExitStack() as es1:
            wd = sb(es1, "wd", [128, NFF, D], BF16)
            aT = sb(es1, "aT", [128, NFF, 1024], BF16)
            h2n = [sb(es1, f"h2n{i}", [128, D], BF16) for i in range(2)]
            ot = xt
            if VERBOSE:
                print('phaseD sbuf remaining', nc.sbuf_bytes_remaining)
            d_wd = dsem("wd")
            d_o = [dsem(f"o{i}") for i in range(2)]
            cs = csD
            gu_free = ffn["gu_free"]
            o_free = [c_xb[0], c_xb[1]]
            out_deps = []
            ps_g = [ps(es1, f"ps_g{i}", [128, 512]) for i in range(2)]
            ps_u2 = [ps(es1, f"ps_u2{i}", [128, 512]) for i in range(2)]
            ps_f = [ps(es1, f"ps_f{i}", [128, D]) for i in range(2)]
            psg_free = [c_act_psum, c_act_psum]
            psu_free = [c_act_psum, c_act_psum]
            sg_free = [None, None]
            psf_free = [c_psf[0], c_psf[1]]
            pending_w = ffn["pending_w"]
            k = 0
            a_dep = None
            h2T_dep = c_act_psum
            wd_deps = []
            for hf in range(2):
                T0 = hf * 8
                for f in range(NFF):
                    if f + 1 < NFF:
                        pending_w.append(issue_gu(f + 1))
                    elif hf == 0:
                        pending_w.append(issue_gu(0))
                    if hf == 0:
                        wd_deps.append(POOL.dma(wd[:, f, :], w_down_d[f * 128:(f + 1) * 128, :], d_wd,
                                                deps=c_end))
                    sl, wdep = pending_w.pop(0)
                    for tg in range(2):
                        kk = k % 2
                        PE.wait([wdep, h2T_dep, psg_free[kk]])
                        for kc in range(8):
                            mg = PE.op(lambda e: e.matmul(out=ps_g[kk][:], lhsT=gu[sl][:, 0, kc, :],
                                                          rhs=h2T[:, kc, tg * 512:(tg + 1) * 512],
                                                          start=(kc == 0), stop=(kc == 7)), signal=(kc == 7))
                        PE.wait([psu_free[kk]])
                        for kc in range(8):
                            mu = PE.op(lambda e: e.matmul(out=ps_u2[kk][:], lhsT=gu[sl][:, 1, kc, :],
                                                          rhs=h2T[:, kc, tg * 512:(tg + 1) * 512],
                                                          start=(kc == 0), stop=(kc == 7)), signal=(kc == 7))
                        si = ACT.op(lambda e: e.activation(out=sg[kk][:], in_=ps_g[kk][:], func=AF.Silu),
                                    deps=[mg, sg_free[kk]])
                        psg_free[kk] = si
                        a_dep = DVE.op(lambda e: e.tensor_tensor(
                            out=aT[:, f, tg * 512:(tg + 1) * 512], in0=sg[kk][:], in1=ps_u2[kk][:],
                            op=ALU.mult), deps=[si, mu])
                        psu_free[kk] = a_dep
                        sg_free[kk] = a_dep
                        k += 1
                    gu_free[sl] = mu
                for tl in range(8):
                    t = T0 + tl
                    b = tl % 2
                    fr = None
                    if hf == 0:
                        fr = d1_front(8 + tl, h2n, [])
                    PE.wait([a_dep, psf_free[b], (d_wd, d_wd.cnt)])
                    for cg in range(2):
                        for f in range(NFF):
                            mm = PE.op(lambda e: e.matmul(out=ps_f[b][:, cg * 512:(cg + 1) * 512],
                                                          lhsT=aT[:, f, tl * 128:(tl + 1) * 128],
                                                          rhs=wd[:, f, cg * 512:(cg + 1) * 512],
                                                          start=(f == 0), stop=(f == NFF - 1)),
                                       signal=(cg == 1 and f == NFF - 1))
                    if fr is not None:
                        psv_d = ps_g[0][:].bitcast(BF16).rearrange("p (a b) -> p a b", b=128)
                        h2T_dep = d1_pe(fr, tl, h2n, psv_d, [psg_free[0], psg_free[1], mu])
                        psg_free[0] = h2T_dep
                    sy = ACT.op(lambda e: e.activation(out=ot[b][:], in_=ps_f[b][:], func=AF.Square,
                                                       accum_out=ST(5, t)), deps=[mm, o_free[b]])
                    qy = ACT.op(lambda e: e.activation(out=ST(6, t), in_=ST(5, t), func=AF.Sqrt,
                                                       scale=1.0 / D, bias=epsr[:]), deps=[sy])
                    ry = DVE.op(lambda e: e.reciprocal(out=ST(7, t), in_=ST(6, t)), deps=[qy])
                    ty = DVE.op(lambda e: e.scalar_tensor_tensor(
                        out=ot[b][:], in0=ps_f[b][:], scalar=ST(7, t), in1=gB4[:],
                        op0=ALU.mult, op1=ALU.mult), deps=[ry, o_free[b]] + cs)
                    psf_free[b] = ty
                    xa = DVE.op(lambda e: e.tensor_tensor(out=ot[b][:, 0:512], in0=ot[b][:, 0:512],
                                                          in1=bufA[:, t, :], op=ALU.add), deps=[ty])
                    xb = DVE.op(lambda e: e.tensor_tensor(out=ot[b][:, 512:1024], in0=ot[b][:, 512:1024],
                                                          in1=bufB[:, t, :], op=ALU.add), deps=[ty])
                    SP.dma(out_d[t * 128:(t + 1) * 128, 0:512], ot[b][:, 0:512], d_o[b], deps=[xa])
                    od = SP.dma(out_d[t * 128:(t + 1) * 128, 512:1024], ot[b][:, 512:1024], d_o[b], deps=[xb])
                    o_free[b] = od
                    out_deps.append(od)
            SP.wait(out_deps)
            barrier()
    return nc


def _host_consts():
    invf64 = 500000.0 ** (-np.arange(0, 16, 2, dtype=np.float64) / 16.0)
    invf_hi = invf64.astype(np.float32)
    invf_lo = (invf64 - invf_hi.astype(np.float64)).astype(np.float32)
    invf = np.concatenate([invf_hi, invf_lo])
    invf = np.ascontiguousarray(np.broadcast_to(invf[None, :], (128, 16))).astype(np.float32)
    ident = np.eye(128, dtype=np.float32)
    c = np.arange(128)[:, None, None]
    d = np.arange(NT)[None, :, None]
    r = np.arange(128)[None, None, :]
    delta = 128 * d + r - c
    cnt = ((delta >= 0) & (delta <= 128)).astype(np.float32)
    cnt += ((delta >= 0) & (delta % 4 == 0) & (delta <= 512)).astype(np.float32)
    cnt += ((delta >= 0) & (delta % 16 == 0) & (delta <= 2048)).astype(np.float32)
    maskc = np.ascontiguousarray(cnt.reshape(128, NT * 128)).astype(np.float32)
    j = np.arange(128)[:, None]
    i = np.arange(128)[None, :]
    tri = (j <= i).astype(np.float32)
    return invf, ident, maskc, tri


_NC_CACHE = {}


def kernel(x, positions, pre_mix_norm, w_in, sgu_ln_gain, sgu_ln_bias, sgu_w_spatial, sgu_b_spatial,
           attn_out_norm, sgu_out_norm, w_out, post_mix_norm, pre_ffn_norm, w_gate, w_up, w_down,
           post_ffn_norm):
    f32 = np.float32
    x = np.asarray(x, f32)
    positions = np.asarray(positions, np.int32)
    B = x.shape[0]
    invf, ident, maskc, tri = _host_consts()
    w_gate = np.asarray(w_gate, f32)[0]
    w_up = np.asarray(w_up, f32)[0]
    wg = w_gate.reshape(8, 128, NFF, 128).transpose(2, 1, 0, 3)
    wu = w_up.reshape(8, 128, NFF, 128).transpose(2, 1, 0, 3)
    w_gu = np.ascontiguousarray(np.stack([wg, wu], axis=2)).reshape(NFF, 128, 2 * 8 * 128)
    wsT = np.ascontiguousarray(np.asarray(sgu_w_spatial, f32)[0].transpose(2, 0, 1)).reshape(128, 8 * 128)
    bsp = np.ascontiguousarray(np.asarray(sgu_b_spatial, f32)[0].T)
    g_mix = np.concatenate([np.asarray(attn_out_norm, f32)[0], np.asarray(sgu_out_norm, f32)[0]])[None, :]
    def rep(v):
        return np.ascontiguousarray(np.broadcast_to(v[None, :], (128, v.shape[0]))).astype(f32)

    shared = dict(
        invf=invf, ident=ident, maskc=maskc, tri=tri,
        g_premix=rep(np.asarray(pre_mix_norm, f32)[0]),
        g_postmix=rep(np.asarray(post_mix_norm, f32)[0]),
        g_preffn=rep(np.asarray(pre_ffn_norm, f32)[0]),
        g_postffn=rep(np.asarray(post_ffn_norm, f32)[0]),
        g_mixT=np.ascontiguousarray(g_mix[0].reshape(8, 128).T),
        ln_g=rep(np.asarray(sgu_ln_gain, f32)[0]),
        ln_b=rep(np.asarray(sgu_ln_bias, f32)[0]),
        w_in=np.ascontiguousarray(np.asarray(w_in, f32)[0]),
        wsT=wsT, bsp=bsp,
        w_out=np.ascontiguousarray(np.asarray(w_out, f32)[0]),
        w_gu=w_gu,
        w_down=np.ascontiguousarray(np.asarray(w_down, f32)[0]),
    )
    in_maps = []
    for b in range(B):
        m = dict(shared)
        m["x"] = np.ascontiguousarray(x[b])
        m["pos"] = np.ascontiguousarray(positions[b].reshape(NT, 128).T)
        in_maps.append(m)
    if "nc" not in _NC_CACHE:
        _NC_CACHE["nc"] = build_program()
    nc = _NC_CACHE["nc"]
    res = run_bass_kernel_spmd(nc, in_maps, core_ids=list(range(B)))
    _NC_CACHE["last"] = res
    out = np.stack([np.asarray(r["out"], f32) for r in res.results], axis=0)
    return out
```
